# Optimizing a Trainium2 kernel written in Bass

```python
import math
import jax, jax.numpy as jnp
from jax import lax
import numpy as np

D_MODEL = 2048
BATCH = 1
SEQ = 16384
DEPTH = 2

HEAD_DIM = 128
N_Q_HEADS = 16
N_KV_HEADS = 4
ATTN_WIDTH = N_Q_HEADS * HEAD_DIM
KV_WIDTH = N_KV_HEADS * HEAD_DIM
CONV_WIDTH = 1024
CONV_K = 3
GMLP_WIDTH = 1024
GMLP_GROUPS = 8
GMLP_GROUP_DIM = GMLP_WIDTH // GMLP_GROUPS
CHUNK = 128
BLOCK = 128
WINDOW = 128
ROPE_THETA = 500000.0
ROPE_DIM = HEAD_DIM // 4
N_BRANCHES = 3
LN_EPS = 1e-5
ALPHA = (2.0 * DEPTH) ** 0.25
BETA = (8.0 * DEPTH) ** -0.25

SEG_WIDTHS = (CONV_WIDTH,) * 4 + (GMLP_WIDTH,) * 3 + (ATTN_WIDTH, KV_WIDTH, KV_WIDTH, ATTN_WIDTH) + (D_MODEL,) * N_BRANCHES
SEG_OFFSETS = tuple(int(o) for o in np.cumsum((0,) + SEG_WIDTHS[:-1]))
IN_WIDTH = int(sum(SEG_WIDTHS))

kernel_name = "hybrid_conv_gmlp_swa_encoder"


def layer_norm(x, g, b):
    xf = x.astype(jnp.float32)
    mu = jnp.mean(xf, axis=-1, keepdims=True)
    var = jnp.mean(jnp.square(xf - mu), axis=-1, keepdims=True)
    return ((xf - mu) * lax.rsqrt(var + LN_EPS)).astype(x.dtype) * g + b


def partial_rope(t, positions):
    half = ROPE_DIM // 2
    inv_freq = ROPE_THETA ** (-jnp.arange(half, dtype=jnp.float32) / half)
    ang = positions.astype(jnp.float32)[..., None] * inv_freq
    cos = jnp.cos(ang)[:, :, None, :].astype(t.dtype)
    sin = jnp.sin(ang)[:, :, None, :].astype(t.dtype)
    t1, t2, rest = t[..., :half], t[..., half:ROPE_DIM], t[..., ROPE_DIM:]
    return jnp.concatenate([t1 * cos - t2 * sin, t2 * cos + t1 * sin, rest], axis=-1)


def short_conv_mixer(b_gate, c_gate, h, conv_w):
    y = c_gate * h
    s = y.shape[1]
    yp = jnp.pad(y, ((0, 0), (1, 1), (0, 0)))
    conv = conv_w[0] * yp[:, :s] + conv_w[1] * yp[:, 1:s + 1] + conv_w[2] * yp[:, 2:]
    return b_gate * conv


def chunked_spatial_gating(u, v, ln_g, ln_b, w_s, b_s):
    u = jax.nn.gelu(u)
    v = layer_norm(jax.nn.gelu(v), ln_g, ln_b)
    bsz, s, _ = v.shape
    vc = v.reshape(bsz, s // CHUNK, CHUNK, GMLP_GROUPS, GMLP_GROUP_DIM)
    mixed = jnp.einsum('gpq,bnqgc->bnpgc', w_s, vc) + b_s.T[None, None, :, :, None]
    return u * mixed.reshape(bsz, s, GMLP_WIDTH)


def windowed_sink_attention(q, k, v, sink, positions):
    bsz, s, _ = q.shape
    nb = s // BLOCK
    grp = N_Q_HEADS // N_KV_HEADS
    q = partial_rope(q.reshape(bsz, s, N_Q_HEADS, HEAD_DIM), positions)
    k = partial_rope(k.reshape(bsz, s, N_KV_HEADS, HEAD_DIM), positions)
    v = v.reshape(bsz, s, N_KV_HEADS, HEAD_DIM)
    qb = q.reshape(bsz, nb, BLOCK, N_KV_HEADS, grp, HEAD_DIM)

    def band(t):
        tb = t.reshape(bsz, nb, BLOCK, N_KV_HEADS, HEAD_DIM)
        tp = jnp.pad(tb, ((0, 0), (1, 1), (0, 0), (0, 0), (0, 0)))
        return jnp.concatenate([tp[:, :-2], tp[:, 1:-1], tp[:, 2:]], axis=2)

    kb, vb = band(k), band(v)
    scores = jnp.einsum('bnqhgd,bnkhd->bnhgqk', qb, kb).astype(jnp.float32) * (HEAD_DIM ** -0.5)
    blk = jnp.arange(nb)[:, None, None]
    qpos = blk * BLOCK + jnp.arange(BLOCK)[None, :, None]
    kpos = (blk - 1) * BLOCK + jnp.arange(3 * BLOCK)[None, None, :]
    valid = (jnp.abs(qpos - kpos) <= WINDOW) & (kpos >= 0) & (kpos < s)
    scores = jnp.where(valid[None, :, None, None], scores, -jnp.inf)
    sink_l = sink.astype(jnp.float32).reshape(N_KV_HEADS, grp)[None, None, :, :, None, None]
    m = jnp.maximum(jnp.max(scores, axis=-1, keepdims=True), sink_l)
    p = jnp.exp(scores - m)
    denom = jnp.sum(p, axis=-1, keepdims=True) + jnp.exp(sink_l - m)
    probs = (p / denom).astype(vb.dtype)
    out = jnp.einsum('bnhgqk,bnkhd->bnqhgd', probs, vb)
    return out.reshape(bsz, s, ATTN_WIDTH)


def hybrid_layer(x, positions, w_in, conv_w, gmlp_ln_g, gmlp_ln_b, spatial_w, spatial_b, sink,
                 w_branch_a, w_branch_b, w_branch_c, gate_b, w_out, ln_g, ln_b):
    (a_b, a_c, a_h, a_z, g_u, g_v, g_z, q, k, v, c_z, r_a, r_b, r_c) = [
        jnp.einsum('bsd,de->bse', x, w_in[:, o:o + w]) for o, w in zip(SEG_OFFSETS, SEG_WIDTHS)]
    y_a = short_conv_mixer(a_b, a_c, a_h, conv_w) * jax.nn.silu(a_z)
    y_b = chunked_spatial_gating(g_u, g_v, gmlp_ln_g, gmlp_ln_b, spatial_w, spatial_b) * jax.nn.silu(g_z)
    y_c = windowed_sink_attention(q, k, v, sink, positions) * jax.nn.silu(c_z)
    merged = (jax.nn.sigmoid(r_a + gate_b[0]) * jnp.einsum('bsc,cd->bsd', y_a, w_branch_a)
              + jax.nn.sigmoid(r_b + gate_b[1]) * jnp.einsum('bsc,cd->bsd', y_b, w_branch_b)
              + jax.nn.sigmoid(r_c + gate_b[2]) * jnp.einsum('bsc,cd->bsd', y_c, w_branch_c))
    out = jnp.einsum('bsd,de->bse', merged, w_out)
    return layer_norm(ALPHA * x + out, ln_g, ln_b)


def setup_inputs(seed: int = 0) -> dict:
    key = jax.random.key(seed)
    ks = jax.random.split(key, 18)
    f32 = jnp.float32
    nrm = lambda k, shape, scale: jax.random.normal(k, shape, f32) * scale
    L = DEPTH
    x = jax.random.normal(ks[0], (BATCH, SEQ, D_MODEL), f32)
    offset = jax.random.randint(ks[1], (BATCH, 1), 0, 1024, dtype=jnp.int32)
    positions = (jnp.arange(SEQ, dtype=jnp.int32)[None, :] + offset).astype(jnp.int32)
    return {
        "x": x,
        "positions": positions,
        "ln0_g": 1.0 + nrm(ks[2], (D_MODEL,), 0.02),
        "ln0_b": nrm(ks[3], (D_MODEL,), 0.02),
        "w_in": nrm(ks[4], (L, D_MODEL, IN_WIDTH), D_MODEL ** -0.5),
        "conv_w": nrm(ks[5], (L, CONV_K, CONV_WIDTH), CONV_K ** -0.5),
        "gmlp_ln_g": 1.0 + nrm(ks[6], (L, GMLP_WIDTH), 0.02),
        "gmlp_ln_b": nrm(ks[7], (L, GMLP_WIDTH), 0.02),
        "spatial_w": nrm(ks[8], (L, GMLP_GROUPS, CHUNK, CHUNK), CHUNK ** -0.5),
        "spatial_b": 1.0 + nrm(ks[9], (L, GMLP_GROUPS, CHUNK), 0.02),
        "sink": nrm(ks[10], (L, N_Q_HEADS), 0.5),
        "w_branch_a": nrm(ks[11], (L, CONV_WIDTH, D_MODEL), BETA * CONV_WIDTH ** -0.5),
        "w_branch_b": nrm(ks[12], (L, GMLP_WIDTH, D_MODEL), BETA * GMLP_WIDTH ** -0.5),
        "w_branch_c": nrm(ks[13], (L, ATTN_WIDTH, D_MODEL), BETA * ATTN_WIDTH ** -0.5),
        "gate_b": nrm(ks[14], (L, N_BRANCHES, D_MODEL), 0.02),
        "w_out": nrm(ks[15], (L, D_MODEL, D_MODEL), BETA * D_MODEL ** -0.5),
        "ln_g": 1.0 + nrm(ks[16], (L, D_MODEL), 0.02),
        "ln_b": nrm(ks[17], (L, D_MODEL), 0.02),
    }


def reference(x, positions, ln0_g, ln0_b, w_in, conv_w, gmlp_ln_g, gmlp_ln_b, spatial_w, spatial_b,
              sink, w_branch_a, w_branch_b, w_branch_c, gate_b, w_out, ln_g, ln_b):
    h = layer_norm(x, ln0_g, ln0_b)
    for l in range(DEPTH):
        h = hybrid_layer(h, positions, w_in[l], conv_w[l], gmlp_ln_g[l], gmlp_ln_b[l], spatial_w[l],
                         spatial_b[l], sink[l], w_branch_a[l], w_branch_b[l], w_branch_c[l], gate_b[l],
                         w_out[l], ln_g[l], ln_b[l])
    return h
```

```python
import math
from contextlib import ExitStack

import numpy as np
import concourse.bass as bass
import concourse.mybir as mybir
from concourse.bass_utils import run_bass_kernel_spmd

F32 = mybir.dt.float32
BF16 = mybir.dt.bfloat16
I32 = mybir.dt.int32
AF = mybir.ActivationFunctionType
ALU = mybir.AluOpType

D = 2048
SEQ = 16384
NCORE = 8
INW = 18432
OFF = dict(ab=0, ac=1024, ah=2048, az=3072, gu=4096, gv=5120, gz=6144, q=7168, k=9216, v=9728,
           cz=10240, ra=12288, rb=14336, rc=16384)
ALPHA = 4.0 ** 0.25
NSLOT = 96
LN_EPS = 1e-5
SCALE = 128.0 ** -0.5
NEG = -30000.0
TWO_PI = 2.0 * math.pi
PI_SAFE = 3.1415925


class Buf:
    __slots__ = ("w", "r", "name")

    def __init__(self, name="", dep=None):
        self.w = {}
        self.r = dict(dep) if dep else {}
        self.name = name


def _merge(d, s):
    for k, v in s.items():
        if d.get(k, 0) < v:
            d[k] = v


class Prog:
    ENG = ("pe", "act", "dve", "pool", "sp")

    def __init__(self):
        self.ops = {e: [] for e in self.ENG}
        self.cnt = {e: 0 for e in self.ENG}
        self.seen = {e: {} for e in self.ENG}
        self.dcnt = {}

    def _waits(self, eng, reads, writes):
        d = {}
        for b in reads:
            _merge(d, b.w)
        for b in writes:
            _merge(d, b.w)
            _merge(d, b.r)
        out = []
        seen = self.seen[eng]
        for k, v in d.items():
            if k == "pe" and eng == "pe":
                continue
            if seen.get(k, 0) >= v:
                continue
            seen[k] = v
            out.append((k, v))
        return out

    def op(self, eng, fn, reads=(), writes=(), inc=True):
        waits = self._waits(eng, reads, writes)
        if inc:
            self.cnt[eng] += 1
            tick = self.cnt[eng]
        else:
            tick = self.cnt[eng] + 1
        self.ops[eng].append((waits, fn, inc))
        for b in reads:
            if b.r.get(eng, 0) < tick:
                b.r[eng] = tick
        for b in writes:
            b.w = {eng: tick}
            b.r = {}

    def dma(self, q, pairs, key, reads=(), writes=()):
        waits = self._waits(q, reads, writes)
        self.dcnt[key] = self.dcnt.get(key, 0) + 16 * len(pairs)
        tick = self.dcnt[key]
        self.ops[q].append((waits, ("dma", pairs, key), False))
        for b in reads:
            if b.r.get(key, 0) < tick:
                b.r[key] = tick
        for b in writes:
            b.w = {key: tick}
            b.r = {}

    def final_wait(self, q, keys):
        self.ops[q].append(([(k, self.dcnt[k]) for k in keys if k in self.dcnt], None, False))

    def all_keys(self):
        return [e for e in ("pe", "act", "dve")] + sorted(self.dcnt.keys())

    def replay(self, block, sems):
        def mk(name):
            def body(e):
                for waits, fn, inc in self.ops[name]:
                    for k, v in waits:
                        e.wait_ge(sems[k], v)
                    if fn is None:
                        continue
                    if isinstance(fn, tuple):
                        _, pairs, key = fn
                        for o, i in pairs:
                            e.dma_start(out=o, in_=i).then_inc(sems[key], 16)
                    else:
                        ins = fn(e)
                        if inc:
                            ins.then_inc(sems[name], 1)
            return body

        block.tensor(mk("pe"))
        block.scalar(mk("act"))
        block.vector(mk("dve"))
        block.gpsimd(mk("pool"))
        block.sync(mk("sp"))


def build_nc(layers=(0, 1), debug=False, dbg_layer=0, h1_kind="Internal"):
    nc = bass.Bass("TRN2", target_bir_lowering=False)
    P = Prog()

    def din(name, shape, dt=F32):
        return nc.dram_tensor(name, list(shape), dt, kind="ExternalInput").ap()

    xs = din("xs", [20 * 128, D])
    posT = din("posT", [128, 20], I32)
    invf_in = din("invf", [128, 16])
    mask_in = din("maskin", [128, 4, 512])
    edge_in = din("edgein", [128, 2])
    ident_in = din("identin", [128, 128])
    ln0g_in = din("ln0_g", [D])
    ln0b_in = din("ln0_b", [D])
    wstream = din("wstream", [2, NSLOT, 128, 4096])
    convw_in = din("conv_wT", [2, 128, 8, 3])
    glg_in = din("glgT", [2, 128, 8])
    glb_in = din("glbT", [2, 128, 8])
    ws_in = din("wsT", [2, 128, 8, 128])
    bs_in = din("spatial_b", [2, 1024])
    sink_in = din("sink", [2, 16])
    gtb_in = din("gtbT", [2, 128, 48])
    lng_in = din("ln_g", [2, D])
    lnb_in = din("ln_b", [2, D])
    h0s = nc.dram_tensor("h0s", [18 * 128, D], F32, kind="Internal").ap()
    if debug:
        h1_kind = "ExternalOutput"
    h1s = nc.dram_tensor("h1s", [18 * 128, D], F32, kind=h1_kind).ap()
    out = nc.dram_tensor("out", [16 * 128, D], F32, kind="ExternalOutput").ap() if (1 in layers) else None
    h0s_b = [Buf("h0s%d" % i) for i in range(18)]
    h1s_b = [Buf("h1s%d" % i) for i in range(18)]

    es = ExitStack()
    with es:
        def sb(name, shape, dt):
            return es.enter_context(nc.sbuf_tensor(name, list(shape), dt))

        ident = sb("ident", [128, 128], BF16)
        ones_bf = sb("ones_bf", [128, 128], BF16)
        masks = sb("masks", [128, 4, 512], BF16)
        edge = sb("edge", [128, 2], F32)
        invf = sb("invf_s", [128, 16], F32)
        posi = sb("posi", [128, 20], I32)
        posf = sb("posf", [128, 20], F32)
        CC = sb("CC", [128, 20, 32], F32)
        SS = sb("SS", [128, 20, 32], F32)
        ang = sb("ang", [128, 20, 16], F32)
        a1 = sb("a1", [128, 20, 16], F32)
        a2 = sb("a2", [128, 20, 16], F32)
        tq = sb("tq", [128, 20, 16], F32)
        ki = sb("ki", [128, 20, 16], I32)
        cw = sb("cw", [128, 8, 3], F32)
        glg = sb("glg", [128, 8], F32)
        glb = sb("glb", [128, 8], F32)
        gtb = sb("gtb", [128, 48], F32)
        wsT = sb("wsT_s", [128, 8, 128], BF16)
        bsb = sb("bsb", [128, 8, 128], F32)
        Rt = sb("Rt", [128, 8, 128], F32)
        skb = sb("skb", [128, 16], F32)
        esk = sb("esk", [128, 16], F32)
        st6 = sb("st6", [128, 8, 4, 6], F32)
        mv = sb("mv", [128, 8, 2], F32)
        ve = sb("ve", [128, 8], F32)
        sq = sb("sq", [128, 8], F32)
        rstd = sb("rstd", [128, 8], F32)
        nbias = sb("nbias", [128, 8], F32)
        hst6 = sb("hst6", [128, 3, 4, 6], F32)
        hmv = sb("hmv", [128, 3, 2], F32)
        hsm = sb("hsm", [128, 3, 4], F32)
        slots = [sb("slot%d" % i, [128, 4096], BF16) for i in range(3)]
        hT = sb("hT", [128, 16, 1024], BF16)
        yT = sb("yT", [128, 32, 768], BF16)
        mT = sb("mT", [128, 16, 768], BF16)
        Fr = sb("Fr", [128, 24576], BF16)
        psum = [es.enter_context(nc.psum_tensor("ps%d" % i, [128, 1024], F32)) for i in range(4)]

        B = {n: Buf(n) for n in ("ident", "ones", "masks", "edge", "invf", "posi", "posf", "CC", "SS", "ang", "a1",
                                 "a2", "tq", "ki", "cw", "glg", "glb", "gtb", "wsT", "bsb", "Rt", "skb", "esk",
                                 "st6", "mv", "ve", "sq", "rstd", "nbias", "hT", "yT", "mT")}
        slot_b = [Buf("slot%d" % i) for i in range(3)]
        hs_b = [Buf("hs0"), Buf("hs1"), Buf("hs2")]
        ps_b = [Buf("ps%d" % i) for i in range(4)]
        zb_b = [Buf("zb%d" % i) for i in range(6)]
        F_live = []

        def fview(off, n, dt, **kw):
            ap = Fr[:, off:off + n]
            if dt != BF16:
                ap = ap.bitcast(dt)
            return ap

        def new_phase(names):
            dep = {}
            for b in F_live:
                _merge(dep, b.w)
                _merge(dep, b.r)
            del F_live[:]
            res = {}
            for n in names:
                res[n] = Buf(n, dep)
                F_live.append(res[n])
            return res

        uidx = [0]

        def next_unit():
            i = uidx[0] % 4
            uidx[0] += 1
            return psum[i], ps_b[i]

        sidx = [0]

        slot_specs = {}
        tile_n = [0]
        cur = {"l": 0, "rec": False}

        def load_slot(specs):
            i = sidx[0] % 3
            sidx[0] += 1
            n = tile_n[0]
            tile_n[0] += 1
            lst = slot_specs.setdefault(cur["l"], [])
            if cur["rec"]:
                lst.append(list(specs))
            else:
                assert lst[n] == list(specs), "slot sequence differs between tiles"
            src = wstream[cur["l"], n]
            pairs = [(slots[i][:, h * 2048:(h + 1) * 2048], src[:, h * 2048:(h + 1) * 2048]) for h in range(2)]
            P.dma("pool", pairs, "w%d" % i, writes=[slot_b[i]])
            return slots[i], slot_b[i]

        def s3(slot):
            return slot[:, :].rearrange("p (k c) -> p k c", c=128)

        def s2(slot):
            return slot[:, :].rearrange("p (k c) -> p k c", c=256)

        def mm_group(out_ap, pairs, pb, reads):
            n = len(pairs)
            for i, (l, r) in enumerate(pairs):
                P.op("pe", lambda e, l=l, r=r, i=i: e.matmul(out_ap, l, r, start=(i == 0), stop=(i == n - 1)),
                     reads=reads, writes=[pb], inc=(i == n - 1))

        def act(out, in_, func, reads, writes, **kw):
            P.op("act", lambda e: e.activation(out=out, in_=in_, func=func, **kw), reads=reads, writes=writes)

        def dve(fn, reads, writes):
            P.op("dve", fn, reads=reads, writes=writes)

        def tt(out, in0, in1, op, reads, writes):
            dve(lambda e: e.tensor_tensor(out=out, in0=in0, in1=in1, op=op), reads, writes)

        def stt(out, in0, scalar, in1, op0, op1, reads, writes):
            dve(lambda e: e.scalar_tensor_tensor(out=out, in0=in0, scalar=scalar, in1=in1, op0=op0, op1=op1),
                reads, writes)

        def ts(out, in0, s1, s2_, op0, op1, reads, writes):
            if s2_ is None:
                dve(lambda e: e.tensor_scalar(out=out, in0=in0, scalar1=s1, scalar2=None, op0=op0), reads, writes)
            else:
                dve(lambda e: e.tensor_scalar(out=out, in0=in0, scalar1=s1, scalar2=s2_, op0=op0, op1=op1),
                    reads, writes)

        def cp(out, in_, reads, writes):
            dve(lambda e: e.tensor_copy(out=out, in_=in_), reads, writes)

        P.dma("sp", [(edge[:], edge_in), (invf[:], invf_in), (posi[:], posT)], "cst",
              writes=[B["edge"], B["invf"], B["posi"]])
        P.dma("pool", [(ident[:], ident_in), (masks[:], mask_in)], "cstp", writes=[B["ident"], B["masks"]])
        dve(lambda e: e.memset(ones_bf[:], 1.0), [], [B["ones"]])
        cp(posf[:], posi[:], [B["posi"]], [B["posf"]])
        tt(ang[:], posf[:].unsqueeze(2).broadcast_to([128, 20, 16]), invf[:].unsqueeze(1).broadcast_to([128, 20, 16]),
           ALU.mult, [B["posf"], B["invf"]], [B["ang"]])
        for (dst, dn, shift) in ((a1, "a1", 0.0), (a2, "a2", math.pi / 2)):
            ts(dst[:], ang[:], shift, None, ALU.add, None, [B["ang"]], [B[dn]])
            ts(tq[:], dst[:], 1.0 / TWO_PI, None, ALU.mult, None, [B[dn]], [B["tq"]])
            cp(ki[:], tq[:], [B["tq"]], [B["ki"]])
            cp(tq[:], ki[:], [B["ki"]], [B["tq"]])
            stt(dst[:], tq[:], -TWO_PI, dst[:], ALU.mult, ALU.add, [B["tq"], B[dn]], [B[dn]])
            ts(tq[:], dst[:], math.pi, -TWO_PI, ALU.is_gt, ALU.mult, [B[dn]], [B["tq"]])
            tt(dst[:], dst[:], tq[:], ALU.add, [B[dn], B["tq"]], [B[dn]])
            ts(dst[:], dst[:], -PI_SAFE, PI_SAFE, ALU.max, ALU.min, [B[dn]], [B[dn]])
        act(SS[:, :, 0:16], a1[:], AF.Sin, [B["a1"]], [B["SS"]], scale=-1.0)
        act(SS[:, :, 16:32], a1[:], AF.Sin, [B["a1"]], [B["SS"]])
        act(CC[:, :, 0:16], a2[:], AF.Sin, [B["a2"]], [B["CC"]])
        act(CC[:, :, 16:32], a2[:], AF.Sin, [B["a2"]], [B["CC"]])

        store_keys = set()
        taps = {}
        cur_layer = [0]

        def tap(name, ap2d, bufs):
            if not debug or name in taps or cur_layer[0] != dbg_layer:
                return
            t = nc.dram_tensor("dbg_" + name, [128, ap2d.shape[1]], ap2d.dtype, kind="ExternalOutput").ap()
            taps[name] = t
            P.dma("sp", [(t, ap2d)], "dbg", reads=bufs)
            store_keys.add("dbg")

        for l in layers:
            first = (l == 0)
            cur_layer[0] = l
            src = xs if first else h1s
            tiles = [(1, 6), (7, 6), (13, 6)] if first else [(1, 6), (7, 6), (13, 4)]
            pboff = 0 if first else 1
            tl_edge = 255 if first else 127
            th_edge = 2304 if first else 2176
            cur["l"] = l
            P.dma("sp", [(cw[:], convw_in[l]), (glg[:], glg_in[l]), (glb[:], glb_in[l]), (gtb[:], gtb_in[l]),
                         (bsb[:].rearrange("p a b -> p (a b)"), bs_in[l].partition_broadcast(128)),
                         (skb[:], sink_in[l].partition_broadcast(128))], "par%d" % l,
                  writes=[B["cw"], B["glg"], B["glb"], B["gtb"], B["bsb"], B["skb"]])
            P.dma("pool", [(wsT[:], ws_in[l])], "parp%d" % l, writes=[B["wsT"]])
            act(esk[:], skb[:], AF.Exp, [B["skb"]], [B["esk"]])
            ps, pb = next_unit()
            for g in range(8):
                P.op("pe", lambda e, g=g, ps=ps: e.matmul(ps[:, g * 128:(g + 1) * 128], ones_bf[:], wsT[:, g, :],
                                                          start=True, stop=True),
                     reads=[B["ones"], B["wsT"]], writes=[pb], inc=(g == 7))
            for g in range(8):
                stt(Rt[:, g, :], ps[:, g * 128:(g + 1) * 128], glb[:, g:g + 1], bsb[:, g, :], ALU.mult, ALU.add,
                    [pb, B["glb"], B["bsb"]], [B["Rt"]])

            for ti, (a, nb) in enumerate(tiles):
                T = nb * 128
                tile_n[0] = 0
                cur["rec"] = (ti == 0)
                nh = nb + 2
                if nb == 6:
                    csub = [(128, 384), (512, 384)]
                    xsub = [(127, 385), (512, 385)]
                else:
                    csub = [(128, 512)]
                    xsub = [(127, 257), (384, 257)]
                nsub = len(csub)

                def psv(ps, subs):
                    n = subs[0][1]
                    return ps[:, :].rearrange("p (j n) -> p j n", n=512)[:, 0:len(subs), 0:n]

                def sbv(ap, subs):
                    n = subs[0][1]
                    return ap.rearrange("p (j n) -> p j n", n=n)

                def fm_unit(slot, sbuf_, blk0, nk, rhs_fn, subs, reads):
                    ps, pb = next_unit()
                    s = s3(slot)
                    for j, (t0, n) in enumerate(subs):
                        pairs = [(s[:, blk0 + kc, :], rhs_fn(kc, t0, n)) for kc in range(nk)]
                        mm_group(ps[:, j * 512:j * 512 + n], pairs, pb, reads + [sbuf_])
                    return ps, pb

                def hrhs(kc, t0, n):
                    return hT[:, kc, t0:t0 + n]

                def wchunk(c0, blk0):
                    return [(128, blk0, 0, 128, "w_in", c0, 16)]

                Fb = new_phase(["xb0", "xb1", "xb2", "xbf0", "xbf1", "l0g", "l0b"])
                xbs = [fview(0, 4096, F32), fview(4096, 4096, F32), fview(8192, 4096, F32)]
                xbfs = [fview(12288, 2048, BF16), fview(14336, 2048, BF16)]
                l0g = fview(16384, 4096, F32)
                l0b = fview(20480, 4096, F32)
                if first:
                    P.dma("sp", [(l0g, ln0g_in.partition_broadcast(128)), (l0b, ln0b_in.partition_broadcast(128))],
                          "l0p", writes=[Fb["l0g"], Fb["l0b"]])
                def head_vars(hb):
                    par = hb % 3
                    return (a - 1 + hb, par, xbs[par], Fb["xb%d" % par], xbfs[hb % 2], Fb["xbf%d" % (hb % 2)], hs_b[par])

                def head_stage1(hb):
                    sblk, par, xb, xbB, xbf, xbfB, hsB = head_vars(hb)
                    rd = [] if first else [h1s_b[sblk]]
                    P.dma("sp", [(xb, src[sblk * 128:(sblk + 1) * 128, :])], "xl%d" % par, reads=rd, writes=[xbB])
                    if first:
                        for j in range(4):
                            dve(lambda e, j=j, xb=xb, par=par: e.bn_stats(out=hst6[:, par, j, :],
                                                                          in_=xb[:, j * 512:(j + 1) * 512]),
                                [xbB], [hsB])
                        dve(lambda e, par=par: e.bn_aggr(out=hmv[:, par, :], in_=hst6[:, par, :, :]), [hsB], [hsB])
                        ts(hsm[:, par, 0:1], hmv[:, par, 1:2], LN_EPS, None, ALU.add, None, [hsB], [hsB])
                        act(hsm[:, par, 1:2], hsm[:, par, 0:1], AF.Sqrt, [hsB], [hsB])
                        dve(lambda e, par=par: e.reciprocal(out=hsm[:, par, 2:3], in_=hsm[:, par, 1:2]), [hsB], [hsB])
                        stt(hsm[:, par, 3:4], hmv[:, par, 0:1], -1.0, hsm[:, par, 2:3], ALU.mult, ALU.mult, [hsB], [hsB])

                def head_stage2a(hb):
                    sblk, par, xb, xbB, xbf, xbfB, hsB = head_vars(hb)
                    if first:
                        act(xb, xb, AF.Identity, [xbB, hsB], [xbB], scale=hsm[:, par, 2:3], bias=hsm[:, par, 3:4])
                        tt(xb, xb, l0g, ALU.mult, [xbB, Fb["l0g"]], [xbB])
                        tt(xb, xb, l0b, ALU.add, [xbB, Fb["l0b"]], [xbB])
                        if 1 <= hb <= nb:
                            P.dma("sp", [(h0s[(sblk - 1) * 128:sblk * 128, :], xb)], "sh%d" % par, reads=[xbB],
                                  writes=[h0s_b[sblk - 1]])
                            store_keys.add("sh%d" % par)
                        act(xbf, xb, AF.Identity, [xbB], [xbfB])
                    else:
                        cp(xbf, xb, [xbB], [xbfB])

                def head_stage2b(hb):
                    sblk, par, xb, xbB, xbf, xbfB, hsB = head_vars(hb)
                    ps, pb = next_unit()
                    psb = ps[:, :].bitcast(BF16).rearrange("p (k c) -> p k c", c=128)
                    for kc in range(16):
                        P.op("pe", lambda e, kc=kc, psb=psb, xbf=xbf: e.transpose(psb[:, kc, :],
                                                                                  xbf[:, kc * 128:(kc + 1) * 128], ident[:]),
                             reads=[xbfB, B["ident"]], writes=[pb], inc=(kc == 15))
                    act(hT[:, :, hb * 128:(hb + 1) * 128], psb[:, 0:16, :], AF.Identity, [pb], [B["hT"]])

                head_stage1(0)
                if nh > 1:
                    head_stage1(1)
                head_stage2a(0)
                for hb in range(nh):
                    if hb + 2 < nh:
                        head_stage1(hb + 2)
                    if hb + 1 < nh:
                        head_stage2a(hb + 1)
                    head_stage2b(hb)

                tap("hT", hT[:, :, :].rearrange("p a b -> p (a b)"), [B["hT"]])
                Fb = new_phase(["ac0", "yy0", "cv0", "sz0", "t10", "ac1", "yy1", "cv1", "sz1", "t11"])
                for g in range(8):
                    pr = g % 2
                    base = pr * 8192
                    ac = fview(base, 1600, F32)[:, 0:T + 2]
                    yy = fview(base + 1600, 1600, F32)[:, 0:T + 2]
                    cv = fview(base + 3200, 1536, F32)[:, 0:T]
                    sz = fview(base + 4800, 1536, F32)[:, 0:T]
                    t1 = fview(base + 6400, 1536, F32)[:, 0:T]
                    bac, byy, bcv, bsz, bt1 = (Fb["ac%d" % pr], Fb["yy%d" % pr], Fb["cv%d" % pr], Fb["sz%d" % pr],
                                               Fb["t1%d" % pr])
                    c = g * 128
                    s1, s1b = load_slot(wchunk(OFF["ac"] + c, 0) + wchunk(OFF["ah"] + c, 16))
                    s2_, s2b = load_slot(wchunk(OFF["ab"] + c, 0) + wchunk(OFF["az"] + c, 16))
                    pac, pacb = fm_unit(s1, s1b, 0, 16, hrhs, xsub, [B["hT"]])
                    act(sbv(ac, xsub), psv(pac, xsub), AF.Identity, [pacb], [bac])
                    pah, pahb = fm_unit(s1, s1b, 16, 16, hrhs, xsub, [B["hT"]])
                    tt(sbv(yy, xsub), psv(pah, xsub), sbv(ac, xsub), ALU.mult, [pahb, bac], [byy])
                    t_lo = (a - 1) * 128 + 127
                    for (te, ecol) in ((tl_edge, 0), (th_edge, 1)):
                        j = te - t_lo
                        if 0 <= j < T + 2:
                            ts(yy[:, j:j + 1], yy[:, j:j + 1], edge[:, ecol:ecol + 1], None, ALU.mult, None,
                               [byy, B["edge"]], [byy])
                    ts(cv, yy[:, 0:T], cw[:, g, 0:1], None, ALU.mult, None, [byy, B["cw"]], [bcv])
                    stt(cv, yy[:, 1:T + 1], cw[:, g, 1:2], cv, ALU.mult, ALU.add, [byy, B["cw"], bcv], [bcv])
                    stt(cv, yy[:, 2:T + 2], cw[:, g, 2:3], cv, ALU.mult, ALU.add, [byy, B["cw"], bcv], [bcv])
                    pab, pabb = fm_unit(s2_, s2b, 0, 16, hrhs, csub, [B["hT"]])
                    tt(sbv(t1, csub), psv(pab, csub), sbv(cv, csub), ALU.mult, [pabb, bcv], [bt1])
                    paz, pazb = fm_unit(s2_, s2b, 16, 16, hrhs, csub, [B["hT"]])
                    act(sbv(sz, csub), psv(paz, csub), AF.Silu, [pazb], [bsz])
                    tt(yT[:, g, 0:T], t1, sz, ALU.mult, [bt1, bsz], [B["yT"]])

                Fb = new_phase(["gv", "nrm", "u0", "z0", "m0", "u1", "z1", "m1"])
                gv = fview(0, 12288, F32).rearrange("p (b c) -> p b c", c=1024)
                nrm = fview(12288, 6144, BF16).rearrange("p (b c) -> p b c", c=1024)
                for s in range(4):
                    c0 = OFF["gv"] + s * 256
                    sl, slb = load_slot([(256, 0, 0, 256, "w_in", c0, 16)])
                    for b0 in range(0, nb, 4):
                        nbu = min(4, nb - b0)
                        ps, pb = next_unit()
                        for i in range(nbu):
                            tb = 128 + (b0 + i) * 128
                            pairs = [(hT[:, kc, tb:tb + 128], s2(sl)[:, kc, :]) for kc in range(16)]
                            mm_group(ps[:, i * 256:(i + 1) * 256], pairs, pb, [B["hT"], slb])
                        act(gv[:, b0:b0 + nbu, s * 256:(s + 1) * 256],
                            ps[:, 0:nbu * 256].rearrange("p (b c) -> p b c", c=256), AF.Gelu_apprx_tanh, [pb], [Fb["gv"]])
                def g_bufs(g):
                    pr = g % 2
                    u = fview(18432 + pr * 3072, 1536, F32)[:, 0:T]
                    z = fview(18432 + pr * 3072 + 1536, 1536, F32)[:, 0:T]
                    m = fview(pr * 1536, 1536, F32)[:, 0:T]
                    return u, z, m, Fb["u%d" % pr], Fb["z%d" % pr], Fb["m%d" % pr]

                def uz(g):
                    u, z, m, bu, bz, bm = g_bufs(g)
                    c = g * 128
                    sl, slb = load_slot(wchunk(OFF["gu"] + c, 0) + wchunk(OFF["gz"] + c, 16))
                    pu, pub = fm_unit(sl, slb, 0, 16, hrhs, csub, [B["hT"]])
                    act(sbv(u, csub), psv(pu, csub), AF.Gelu_apprx_tanh, [pub], [bu])
                    pz, pzb = fm_unit(sl, slb, 16, 16, hrhs, csub, [B["hT"]])
                    act(sbv(z, csub), psv(pz, csub), AF.Silu, [pzb], [bz])

                def spatial(g):
                    u, z, m, bu, bz, bm = g_bufs(g)
                    psp, pspb = next_unit()
                    for b in range(nb):
                        P.op("pe", lambda e, b=b, g=g, psp=psp: e.matmul(psp[:, b * 128:(b + 1) * 128],
                                                                         nrm[:, b, g * 128:(g + 1) * 128], wsT[:, g, :],
                                                                         start=True, stop=True),
                             reads=[Fb["nrm"], B["wsT"]], writes=[pspb], inc=(b == nb - 1))
                    stt(m.rearrange("p (b c) -> p b c", c=128), psp[:, 0:T].rearrange("p (b c) -> p b c", c=128),
                        glg[:, g:g + 1], Rt[:, g, :].unsqueeze(1).broadcast_to([128, nb, 128]), ALU.mult, ALU.add,
                        [pspb, B["glg"], B["Rt"], Fb["gv"], Fb["nrm"]], [bm, Fb["gv"]])
                    tt(m, m, u, ALU.mult, [bm, bu], [bm])
                    tt(yT[:, 8 + g, 0:T], m, z, ALU.mult, [bm, bz], [B["yT"]])

                uz(0)
                for b in range(nb):
                    for j in range(2):
                        dve(lambda e, b=b, j=j: e.bn_stats(out=st6[:, b, j, :], in_=gv[:, b, j * 512:(j + 1) * 512]),
                            [Fb["gv"]], [B["st6"]])
                    dve(lambda e, b=b: e.bn_aggr(out=mv[:, b, :], in_=st6[:, b, 0:2, :]), [B["st6"]], [B["mv"]])
                ts(ve[:, 0:nb], mv[:, 0:nb, 1], LN_EPS, None, ALU.add, None, [B["mv"]], [B["ve"]])
                act(sq[:, 0:nb], ve[:, 0:nb], AF.Sqrt, [B["ve"]], [B["sq"]])
                dve(lambda e, nb=nb: e.reciprocal(out=rstd[:, 0:nb], in_=sq[:, 0:nb]), [B["sq"]], [B["rstd"]])
                stt(nbias[:, 0:nb], mv[:, 0:nb, 0], -1.0, rstd[:, 0:nb], ALU.mult, ALU.mult, [B["mv"], B["rstd"]],
                    [B["nbias"]])
                for b in range(nb):
                    act(nrm[:, b, :], gv[:, b, :], AF.Identity, [Fb["gv"], B["rstd"], B["nbias"]], [Fb["nrm"]],
                        scale=rstd[:, b:b + 1], bias=nbias[:, b:b + 1])
                for g in range(8):
                    if g + 1 < 8:
                        uz(g + 1)
                    spatial(g)

                Fb = new_phase(["kvf0", "kvf1", "kbf", "kT", "qf", "qbf", "qT", "szc", "P0", "P1", "dsb", "o", "tr1",
                                "tr2"])
                kvfs = [fview(0, 2048, F32).rearrange("p (b c) -> p b c", c=256),
                        fview(2048, 2048, F32).rearrange("p (b c) -> p b c", c=256)]
                kbf = fview(4096, 2048, BF16).rearrange("p (b c) -> p b c", c=256)
                kT = fview(6144, 1024, BF16)
                qf = fview(7168, 2048, F32).rearrange("p (b c) -> p b c", c=512)
                qbf = fview(9216, 3072, BF16).rearrange("p (b c) -> p b c", c=512)
                qT = fview(12288, 3072, BF16)
                szc = fview(15360, 3072, BF16)
                Pt = [fview(18432, 1536, BF16), fview(19968, 1536, BF16)]
                dsb = fview(21504, 1024, F32)
                ot = fview(22528, 1024, F32)
                tr1 = fview(23552, 512, F32)
                tr2 = fview(24064, 512, F32)
                for kvh in range(4):
                    ck = OFF["k"] + kvh * 128
                    cvv = OFF["v"] + kvh * 128
                    sl, slb = load_slot([(256, 0, 0, 128, "w_in", ck, 16), (256, 0, 128, 128, "w_in", cvv, 16)])
                    for ui, b0 in enumerate(range(0, nh, 4)):
                        nbu = min(4, nh - b0)
                        kvf = kvfs[ui % 2]
                        bkvf = Fb["kvf%d" % (ui % 2)]
                        ps, pb = next_unit()
                        for i in range(nbu):
                            tb = (b0 + i) * 128
                            pairs = [(hT[:, kc, tb:tb + 128], s2(sl)[:, kc, :]) for kc in range(16)]
                            mm_group(ps[:, i * 256:(i + 1) * 256], pairs, pb, [B["hT"], slb])
                        act(kvf[:, 0:nbu, :], ps[:, 0:nbu * 256].rearrange("p (b c) -> p b c", c=256), AF.Identity, [pb],
                            [bkvf])
                        cp(kbf[:, b0:b0 + nbu, 32:256], kvf[:, 0:nbu, 32:256], [bkvf], [Fb["kbf"]])
                        tb0 = a - 1 + b0 + pboff
                        r1 = tr1[:, 0:nbu * 32].rearrange("p (b c) -> p b c", c=32)
                        r2 = tr2[:, 0:nbu * 32].rearrange("p (b c) -> p b c", c=32)
                        tt(r1, kvf[:, 0:nbu, 0:32], CC[:, tb0:tb0 + nbu, :], ALU.mult, [bkvf, B["CC"]], [Fb["tr1"]])
                        tt(r2[:, :, 0:16], kvf[:, 0:nbu, 16:32], SS[:, tb0:tb0 + nbu, 0:16], ALU.mult, [bkvf, B["SS"]],
                           [Fb["tr2"]])
                        tt(r2[:, :, 16:32], kvf[:, 0:nbu, 0:16], SS[:, tb0:tb0 + nbu, 16:32], ALU.mult, [bkvf, B["SS"]],
                           [Fb["tr2"]])
                        tt(kbf[:, b0:b0 + nbu, 0:32], r1, r2, ALU.add, [Fb["tr1"], Fb["tr2"]], [Fb["kbf"]])
                    qs = []
                    for s in range(2):
                        c0 = OFF["q"] + kvh * 512 + s * 256
                        qs.append(load_slot([(256, 0, 0, 256, "w_in", c0, 16)]))
                    for b0 in range(0, nb, 2):
                        ps, pb = next_unit()
                        for i in range(2):
                            tb = 128 + (b0 + i) * 128
                            for s in range(2):
                                pairs = [(hT[:, kc, tb:tb + 128], s2(qs[s][0])[:, kc, :]) for kc in range(16)]
                                mm_group(ps[:, i * 512 + s * 256:i * 512 + (s + 1) * 256], pairs, pb,
                                         [B["hT"], qs[s][1]])
                        act(qf, ps[:, :].rearrange("p (b c) -> p b c", c=512), AF.Identity, [pb], [Fb["qf"]])
                        qf4 = qf.rearrange("p b (h d) -> p b h d", d=128)
                        qb4 = qbf[:, b0:b0 + 2, :].rearrange("p b (h d) -> p b h d", d=128)
                        cp(qb4[:, :, :, 32:128], qf4[:, :, :, 32:128], [Fb["qf"]], [Fb["qbf"]])
                        tb0 = a + b0 + pboff
                        r1 = tr1[:, 0:256].rearrange("p (b h c) -> p b h c", h=4, c=32)
                        r2 = tr2[:, 0:256].rearrange("p (b h c) -> p b h c", h=4, c=32)
                        CCb = CC[:, tb0:tb0 + 2, :].unsqueeze(2).broadcast_to([128, 2, 4, 32])
                        SSb = SS[:, tb0:tb0 + 2, :].unsqueeze(2).broadcast_to([128, 2, 4, 32])
                        tt(r1, qf4[:, :, :, 0:32], CCb, ALU.mult, [Fb["qf"], B["CC"]], [Fb["tr1"]])
                        tt(r2[:, :, :, 0:16], qf4[:, :, :, 16:32], SSb[:, :, :, 0:16], ALU.mult, [Fb["qf"], B["SS"]],
                           [Fb["tr2"]])
                        tt(r2[:, :, :, 16:32], qf4[:, :, :, 0:16], SSb[:, :, :, 16:32], ALU.mult, [Fb["qf"], B["SS"]],
                           [Fb["tr2"]])
                        tt(qb4[:, :, :, 0:32], r1, r2, ALU.add, [Fb["tr1"], Fb["tr2"]], [Fb["qbf"]])
                    nbs = nb // nsub
                    szc5 = szc[:, 0:nb * 512].rearrange("p (j b h q) -> p j b h q", j=nsub, b=nbs, h=4)
                    for s in range(2):
                        c0 = OFF["cz"] + kvh * 512 + s * 256
                        sl, slb = load_slot(wchunk(c0, 0) + wchunk(c0 + 128, 16))
                        for hh in range(2):
                            h = s * 2 + hh
                            pc, pcb = fm_unit(sl, slb, hh * 16, 16, hrhs, csub, [B["hT"]])
                            act(szc5[:, :, :, h, :], psv(pc, csub).rearrange("p j (b q) -> p j b q", q=128), AF.Silu,
                                [pcb], [Fb["szc"]])
                    ps, pb = next_unit()
                    psb = ps[:, :].bitcast(BF16).rearrange("p (k c) -> p k c", c=128)
                    for hb in range(nh):
                        P.op("pe", lambda e, hb=hb, psb=psb: e.transpose(psb[:, hb, :], kbf[:, hb, 0:128], ident[:]),
                             reads=[Fb["kbf"], B["ident"]], writes=[pb], inc=(hb == nh - 1))
                    cp(kT[:, 0:nh * 128], ps[:, :].bitcast(BF16)[:, 0:nh * 128], [pb], [Fb["kT"]])
                    for b0 in range(0, nb, 4):
                        nbu = min(4, nb - b0)
                        ps, pb = next_unit()
                        psb = ps[:, :].bitcast(BF16).rearrange("p (k c) -> p k c", c=128)
                        for i in range(nbu):
                            for h in range(4):
                                last = (i == nbu - 1 and h == 3)
                                P.op("pe", lambda e, i=i, h=h, b0=b0, psb=psb: e.transpose(
                                    psb[:, i * 4 + h, :], qbf[:, b0 + i, h * 128:(h + 1) * 128], ident[:]),
                                     reads=[Fb["qbf"], B["ident"]], writes=[pb], inc=last)
                        cp(qT[:, b0 * 512:(b0 + nbu) * 512], ps[:, :].bitcast(BF16)[:, 0:nbu * 512], [pb], [Fb["qT"]])
                    pend = {}
                    for it in range(nb + 1):
                        if it < nb:
                            b = it
                            xloc = a + b + pboff
                            mprev = masks[:, 2, :] if xloc == 2 else masks[:, 0, :]
                            mnext = masks[:, 3, :] if xloc == 17 else masks[:, 1, :]
                            Pb_ = Pt[b % 2]
                            bP = Fb["P%d" % (b % 2)]
                            qrhs = qT[:, b * 512:(b + 1) * 512]
                            psA, pbA = next_unit()
                            mm_group(psA[:, 0:512], [(kT[:, b * 128:(b + 1) * 128], qrhs), (ident[:], mprev)], pbA,
                                     [Fb["kT"], Fb["qT"], B["ident"], B["masks"]])
                            mm_group(psA[:, 512:1024], [(kT[:, (b + 1) * 128:(b + 2) * 128], qrhs)], pbA,
                                     [Fb["kT"], Fb["qT"]])
                            act(Pb_[:, 0:1024], psA[:, :], AF.Exp, [pbA], [bP], scale=SCALE)
                            psB, pbB = next_unit()
                            mm_group(psB[:, 0:512], [(kT[:, (b + 2) * 128:(b + 3) * 128], qrhs), (ident[:], mnext)], pbB,
                                     [Fb["kT"], Fb["qT"], B["ident"], B["masks"]])
                            act(Pb_[:, 1024:1536], psB[:, 0:512], AF.Exp, [pbB], [bP], scale=SCALE)
                        if it >= 1:
                            b = it - 1
                            Pb_ = Pt[b % 2]
                            bP = Fb["P%d" % (b % 2)]
                            ps, pb = next_unit()
                            mm_group(ps[:, 0:512], [(kbf[:, b + j, 128:256], Pb_[:, j * 512:(j + 1) * 512]) for j in range(3)],
                                     pb, [Fb["kbf"], bP])
                            mm_group(ps[:, 512:1024], [(ones_bf[:], Pb_[:, j * 512:(j + 1) * 512]) for j in range(3)],
                                     pb, [B["ones"], bP])
                            tt(ot.rearrange("p (h q) -> p h q", q=128), ps[:, 0:512].rearrange("p (h q) -> p h q", q=128),
                               szc[:, b * 512:(b + 1) * 512].rearrange("p (h q) -> p h q", q=128), ALU.mult,
                               [pb, Fb["szc"]], [Fb["o"]])
                            tt(dsb.rearrange("p (h q) -> p h q", q=128), ps[:, 512:1024].rearrange("p (h q) -> p h q", q=128),
                               esk[:, kvh * 4:kvh * 4 + 4].unsqueeze(2).broadcast_to([128, 4, 128]), ALU.add,
                               [pb, B["esk"]], [Fb["dsb"]])
                            dve(lambda e: e.reciprocal(out=dsb, in_=dsb), [Fb["dsb"]], [Fb["dsb"]])
                            tt(yT[:, 16 + kvh * 4:16 + kvh * 4 + 4, b * 128:(b + 1) * 128],
                               ot.rearrange("p (h q) -> p h q", q=128), dsb.rearrange("p (h q) -> p h q", q=128), ALU.mult,
                               [Fb["o"], Fb["dsb"]], [B["yT"]])

                tap("yT", yT[:, :, :].rearrange("p a b -> p (a b)"), [B["yT"]])
                Fb = new_phase(["sg0", "sg1", "sg2", "sg3", "sg4", "sg5", "mm", "tm"])
                sgs = [fview(i * 3072, 3072, F32)[:, 0:T] for i in range(6)]
                mmt = fview(18432, 3072, F32)[:, 0:T]
                tmt = fview(21504, 3072, F32)[:, 0:T]

                def gate_slot(c0a, c0b):
                    return load_slot(wchunk(c0a, 0) + wchunk(c0b, 16))

                def gate_unit(sl, slb, blk0, si, col):
                    pg, pgb = fm_unit(sl, slb, blk0, 16, hrhs, csub, [B["hT"]])
                    act(sbv(sgs[si], csub), psv(pg, csub), AF.Sigmoid, [pgb, B["gtb"]], [Fb["sg%d" % si]],
                        bias=gtb[:, col:col + 1])

                def branch(j, sis):
                    c = j * 128
                    sl, slb = load_slot([(128, 0, 0, 128, "w_branch_a", c, 8), (128, 8, 0, 128, "w_branch_b", c, 8),
                                         (128, 16, 0, 128, "w_branch_c", c, 16)])
                    first_ = True
                    for (blk0, nk, y0, si) in ((0, 8, 0, sis[0]), (8, 8, 8, sis[1]), (16, 16, 16, sis[2])):
                        pp, ppb = fm_unit(sl, slb, blk0, nk, lambda kc, t0, n, y0=y0: yT[:, y0 + kc, t0 - 128:t0 - 128 + n],
                                          csub, [B["yT"]])
                        dst = mmt if first_ else tmt
                        dstb = Fb["mm"] if first_ else Fb["tm"]
                        tt(sbv(dst, csub), psv(pp, csub), sbv(sgs[si], csub), ALU.mult, [ppb, Fb["sg%d" % si]], [dstb])
                        if not first_:
                            if blk0 == 16:
                                tt(mT[:, j, 0:T], mmt, tmt, ALU.add, [Fb["mm"], Fb["tm"]], [B["mT"]])
                            else:
                                tt(mmt, mmt, tmt, ALU.add, [Fb["mm"], Fb["tm"]], [Fb["mm"]])
                        first_ = False

                for j0 in range(0, 16, 2):
                    ca, cb_, cc = OFF["ra"], OFF["rb"], OFF["rc"]
                    g0 = gate_slot(ca + j0 * 128, cb_ + j0 * 128)
                    gate_unit(g0[0], g0[1], 0, 0, 0 * 16 + j0)
                    gate_unit(g0[0], g0[1], 16, 1, 1 * 16 + j0)
                    g1 = gate_slot(cc + j0 * 128, ca + (j0 + 1) * 128)
                    gate_unit(g1[0], g1[1], 0, 2, 2 * 16 + j0)
                    gate_unit(g1[0], g1[1], 16, 3, 0 * 16 + j0 + 1)
                    branch(j0, (0, 1, 2))
                    g2 = gate_slot(cb_ + (j0 + 1) * 128, cc + (j0 + 1) * 128)
                    gate_unit(g2[0], g2[1], 0, 4, 1 * 16 + j0 + 1)
                    gate_unit(g2[0], g2[1], 16, 5, 2 * 16 + j0 + 1)
                    branch(j0 + 1, (3, 4, 5))

                tap("mT", mT[:, :, :].rearrange("p a b -> p (a b)"), [B["mT"]])
                Fb = new_phase(["lg", "lb"])
                lgt = fview(8192, 4096, F32)
                lbt = fview(12288, 4096, F32)
                P.dma("sp", [(lgt, lng_in[l].partition_broadcast(128)), (lbt, lnb_in[l].partition_broadcast(128))],
                      "lnp", writes=[Fb["lg"], Fb["lb"]])
                zbuf = yT[:, :, :].rearrange("p a b -> p (a b)").bitcast(F32).rearrange("p (b c) -> p b c", c=D)
                for b in range(nb):
                    sblk = a + b
                    if first:
                        P.dma("sp", [(zbuf[:, b, :], h0s[(sblk - 1) * 128:sblk * 128, :])], "rs%d" % b,
                              reads=[h0s_b[sblk - 1]], writes=[B["yT"], zb_b[b]])
                    else:
                        P.dma("sp", [(zbuf[:, b, :], h1s[sblk * 128:(sblk + 1) * 128, :])], "rs%d" % b,
                              reads=[h1s_b[sblk]], writes=[B["yT"], zb_b[b]])

                def t_pass(groups_):
                    for s in range(8):
                        c0 = s * 256
                        sl, slb = load_slot([(256, 0, 0, 256, "w_out", c0, 16)])
                        for (b0, nbu) in groups_:
                            ps, pb = next_unit()
                            for i in range(nbu):
                                tb = (b0 + i) * 128
                                pairs = [(mT[:, kc, tb:tb + 128], s2(sl)[:, kc, :]) for kc in range(16)]
                                mm_group(ps[:, i * 256:(i + 1) * 256], pairs, pb, [B["mT"], slb])
                            zsl = zbuf[:, b0:b0 + nbu, c0:c0 + 256]
                            stt(zsl, zsl, ALPHA, ps[:, 0:nbu * 256].rearrange("p (b c) -> p b c", c=256), ALU.mult, ALU.add,
                                [pb] + zb_b[b0:b0 + nbu], zb_b[b0:b0 + nbu])

                def t_partA(b0, nbu):
                    for b in range(b0, b0 + nbu):
                        zb = zbuf[:, b, :]
                        for j in range(4):
                            dve(lambda e, b=b, j=j, zb=zb: e.bn_stats(out=st6[:, b, j, :], in_=zb[:, j * 512:(j + 1) * 512]),
                                [zb_b[b]], [B["st6"]])
                        dve(lambda e, b=b: e.bn_aggr(out=mv[:, b, :], in_=st6[:, b, :, :]), [B["st6"]], [B["mv"]])

                def t_rstd(b0, nbu):
                    b1 = b0 + nbu
                    ts(ve[:, b0:b1], mv[:, b0:b1, 1], LN_EPS, None, ALU.add, None, [B["mv"]], [B["ve"]])
                    act(sq[:, b0:b1], ve[:, b0:b1], AF.Sqrt, [B["ve"]], [B["sq"]])
                    dve(lambda e, b0=b0, b1=b1: e.reciprocal(out=rstd[:, b0:b1], in_=sq[:, b0:b1]), [B["sq"]], [B["rstd"]])
                    stt(nbias[:, b0:b1], mv[:, b0:b1, 0], -1.0, rstd[:, b0:b1], ALU.mult, ALU.mult, [B["mv"], B["rstd"]],
                        [B["nbias"]])

                def t_partB(b0, nbu):
                    for b in range(b0, b0 + nbu):
                        sblk = a + b
                        zb = zbuf[:, b, :]
                        act(zb, zb, AF.Identity, [zb_b[b], B["rstd"], B["nbias"]], [zb_b[b]], scale=rstd[:, b:b + 1],
                            bias=nbias[:, b:b + 1])
                        tt(zb, zb, lgt, ALU.mult, [zb_b[b], Fb["lg"]], [zb_b[b]])
                        tt(zb, zb, lbt, ALU.add, [zb_b[b], Fb["lb"]], [zb_b[b]])
                        key = "so%d" % b
                        store_keys.add(key)
                        if first:
                            P.dma("sp", [(h1s[(sblk - 1) * 128:sblk * 128, :], zb)], key, reads=[zb_b[b]],
                                  writes=[h1s_b[sblk - 1]])
                        else:
                            P.dma("sp", [(out[(sblk - 1) * 128:sblk * 128, :], zb)], key, reads=[zb_b[b]])

                groups = [(0, 4), (4, 2)] if nb == 6 else [(0, 4)]
                t_pass(groups)
                t_partA(0, nb)
                t_rstd(0, nb)
                t_partB(0, nb)
                for b in range(nb):
                    _merge(B["yT"].r, zb_b[b].r)
                    _merge(B["yT"].r, zb_b[b].w)

        P.final_wait("sp", sorted(store_keys))

        sems = {}
        for k in P.all_keys():
            sems[k] = es.enter_context(nc.semaphore("s_" + k))
        block = es.enter_context(nc.Block())
        P.replay(block, sems)
    for l in layers:
        assert len(slot_specs[l]) == NSLOT, len(slot_specs[l])
    return nc, slot_specs


def _pack_weights(inputs, specs_by_layer):
    ws = np.zeros((2, NSLOT, 128, 4096), np.float32)
    srcs = {k: np.asarray(inputs[k], np.float32) for k in ("w_in", "w_branch_a", "w_branch_b", "w_branch_c", "w_out")}
    for l, specs in specs_by_layer.items():
        for n, sp in enumerate(specs):
            img = ws[l, n]
            for (V, k0, j0, w, src, c0, nk) in sp:
                v = img.reshape(128, 4096 // V, V)
                W = srcs[src][l]
                v[:, k0:k0 + nk, j0:j0 + w] = W[0:nk * 128, c0:c0 + w].reshape(nk, 128, w).transpose(1, 0, 2)
    return ws


def _host_inputs(inputs, specs_by_layer):
    f32 = np.float32
    x = np.asarray(inputs["x"], dtype=f32).reshape(SEQ, D)
    pos = np.asarray(inputs["positions"]).reshape(SEQ).astype(np.int32)
    xp = np.zeros((SEQ + 4 * 128, D), f32)
    xp[256:256 + SEQ] = x
    pp = np.zeros((SEQ + 4 * 128,), np.int32)
    pp[256:256 + SEQ] = pos
    invf = (500000.0 ** (-(np.arange(16, dtype=f32) / f32(16.0)))).astype(f32)
    invf = np.ascontiguousarray(np.broadcast_to(invf, (128, 16)))
    kk = np.arange(128)[:, None]
    qq = np.arange(128)[None, :]
    tri_prev = np.where(kk >= qq, 0.0, NEG).astype(f32)
    tri_next = np.where(kk <= qq, 0.0, NEG).astype(f32)
    allneg = np.full((128, 128), NEG, f32)
    ident = np.eye(128, dtype=f32)
    shared = dict(
        invf=invf, identin=ident,
        ln0_g=np.asarray(inputs["ln0_g"], f32), ln0_b=np.asarray(inputs["ln0_b"], f32),
        wstream=_pack_weights(inputs, specs_by_layer),
        conv_wT=np.ascontiguousarray(np.asarray(inputs["conv_w"], f32).reshape(2, 3, 8, 128).transpose(0, 3, 2, 1)),
        glgT=np.ascontiguousarray(np.asarray(inputs["gmlp_ln_g"], f32).reshape(2, 8, 128).transpose(0, 2, 1)),
        glbT=np.ascontiguousarray(np.asarray(inputs["gmlp_ln_b"], f32).reshape(2, 8, 128).transpose(0, 2, 1)),
        wsT=np.ascontiguousarray(np.asarray(inputs["spatial_w"], f32).transpose(0, 3, 1, 2)),
        spatial_b=np.ascontiguousarray(np.asarray(inputs["spatial_b"], f32).reshape(2, 1024)),
        sink=np.asarray(inputs["sink"], f32),
        gtbT=np.ascontiguousarray(np.asarray(inputs["gate_b"], f32).reshape(2, 3, 16, 128).transpose(0, 3, 1, 2)
                                  .reshape(2, 128, 48)),
        ln_g=np.asarray(inputs["ln_g"], f32), ln_b=np.asarray(inputs["ln_b"], f32),
    )
    maps = []
    for c in range(NCORE):
        t0 = c * 2048
        m = dict(shared)
        m["xs"] = np.ascontiguousarray(xp[t0:t0 + 2560])
        m["posT"] = np.ascontiguousarray(pp[t0:t0 + 2560].reshape(20, 128).T)
        mk = np.stack([np.tile(tri_prev, (1, 4)), np.tile(tri_next, (1, 4)),
                       np.tile(allneg if c == 0 else tri_prev, (1, 4)),
                       np.tile(allneg if c == NCORE - 1 else tri_next, (1, 4))], axis=1)
        m["maskin"] = np.ascontiguousarray(mk.astype(f32))
        e = np.ones((128, 2), f32)
        if c == 0:
            e[:, 0] = 0.0
        if c == NCORE - 1:
            e[:, 1] = 0.0
        m["edgein"] = e
        maps.append(m)
    return maps


FUSED = True


def kernel(**inputs):
    if FUSED:
        nc, specs = build_nc()
        maps = _host_inputs(inputs, specs)
        res = run_bass_kernel_spmd(nc, maps, core_ids=list(range(NCORE)))
    else:
        nc0, sp0 = build_nc(layers=(0,), h1_kind="ExternalOutput")
        nc1, sp1 = build_nc(layers=(1,), h1_kind="ExternalInput")
        specs = dict(sp0)
        specs.update(sp1)
        maps = _host_inputs(inputs, specs)
        r0 = run_bass_kernel_spmd(nc0, maps, core_ids=list(range(NCORE)))
        maps1 = []
        for c in range(NCORE):
            m = dict(maps[c])
            m["h1s"] = np.asarray(r0.results[c]["h1s"], dtype=np.float32)
            maps1.append(m)
        res = run_bass_kernel_spmd(nc1, maps1, core_ids=list(range(NCORE)))
    outs = [np.asarray(r["out"], dtype=np.float32) for r in res.results]
    return np.concatenate(outs, axis=0).reshape(1, SEQ, D)
```

```python
import math
from contextlib import ExitStack

import numpy as np
import concourse.bass as bass
import concourse.mybir as mybir
from concourse.bass_utils import run_bass_kernel_spmd

F32 = mybir.dt.float32
BF16 = mybir.dt.bfloat16
I32 = mybir.dt.int32
AF = mybir.ActivationFunctionType
ALU = mybir.AluOpType

D = 2048
SEQ = 16384
NCORE = 8
INW = 18432
OFF = dict(ab=0, ac=1024, ah=2048, az=3072, gu=4096, gv=5120, gz=6144, q=7168, k=9216, v=9728,
           cz=10240, ra=12288, rb=14336, rc=16384)
ALPHA = 4.0 ** 0.25
NWS = 4
NSLOT = 96
LN_EPS = 1e-5
SCALE = 128.0 ** -0.5
NEG = -30000.0
TWO_PI = 2.0 * math.pi
PI_SAFE = 3.1415925


class Buf:
    __slots__ = ("w", "r", "name")

    def __init__(self, name="", dep=None):
        self.w = {}
        self.r = dict(dep) if dep else {}
        self.name = name


def _merge(d, s):
    for k, v in s.items():
        if d.get(k, 0) < v:
            d[k] = v


class Prog:
    ENG = ("pe", "act", "dve", "pool", "sp")

    def __init__(self):
        self.ops = {e: [] for e in self.ENG}
        self.cnt = {e: 0 for e in self.ENG}
        self.seen = {e: {} for e in self.ENG}
        self.dcnt = {}

    def _waits(self, eng, reads, writes):
        d = {}
        for b in reads:
            _merge(d, b.w)
        for b in writes:
            _merge(d, b.w)
            _merge(d, b.r)
        out = []
        seen = self.seen[eng]
        for k, v in d.items():
            if k == "pe" and eng == "pe":
                continue
            if seen.get(k, 0) >= v:
                continue
            seen[k] = v
            out.append((k, v))
        return out

    def op(self, eng, fn, reads=(), writes=(), inc=True):
        waits = self._waits(eng, reads, writes)
        if inc:
            self.cnt[eng] += 1
            tick = self.cnt[eng]
        else:
            tick = self.cnt[eng] + 1
        self.ops[eng].append((waits, fn, inc))
        for b in reads:
            if b.r.get(eng, 0) < tick:
                b.r[eng] = tick
        for b in writes:
            b.w = {eng: tick}
            b.r = {}

    def dma(self, q, pairs, key, reads=(), writes=()):
        waits = self._waits(q, reads, writes)
        self.dcnt[key] = self.dcnt.get(key, 0) + 16 * len(pairs)
        tick = self.dcnt[key]
        self.ops[q].append((waits, ("dma", pairs, key), False))
        for b in reads:
            if b.r.get(key, 0) < tick:
                b.r[key] = tick
        for b in writes:
            b.w = {key: tick}
            b.r = {}

    def final_wait(self, q, keys):
        self.ops[q].append(([(k, self.dcnt[k]) for k in keys if k in self.dcnt], None, False))

    def all_keys(self):
        return [e for e in ("pe", "act", "dve")] + sorted(self.dcnt.keys())

    def replay(self, block, sems):
        def mk(name):
            def body(e):
                for waits, fn, inc in self.ops[name]:
                    for k, v in waits:
                        e.wait_ge(sems[k], v)
                    if fn is None:
                        continue
                    if isinstance(fn, tuple):
                        _, pairs, key = fn
                        for o, i in pairs:
                            e.dma_start(out=o, in_=i).then_inc(sems[key], 16)
                    else:
                        ins = fn(e)
                        if inc:
                            ins.then_inc(sems[name], 1)
            return body

        block.tensor(mk("pe"))
        block.scalar(mk("act"))
        block.vector(mk("dve"))
        block.gpsimd(mk("pool"))
        block.sync(mk("sp"))


def build_nc(layers=(0, 1), debug=False, dbg_layer=0, h1_kind="Internal"):
    nc = bass.Bass("TRN2", target_bir_lowering=False)
    P = Prog()

    def din(name, shape, dt=F32):
        return nc.dram_tensor(name, list(shape), dt, kind="ExternalInput").ap()

    xs = din("xs", [20 * 128, D])
    posT = din("posT", [128, 20], I32)
    invf_in = din("invf", [128, 16])
    mask_in = din("maskin", [128, 4, 512])
    edge_in = din("edgein", [128, 2])
    ident_in = din("identin", [128, 128])
    ln0g_in = din("ln0_g", [D])
    ln0b_in = din("ln0_b", [D])
    wstream = din("wstream", [2, NSLOT, 128, 4096])
    convw_in = din("conv_wT", [2, 128, 8, 3])
    glg_in = din("glgT", [2, 128, 8])
    glb_in = din("glbT", [2, 128, 8])
    ws_in = din("wsT", [2, 128, 8, 128])
    bs_in = din("spatial_b", [2, 1024])
    sink_in = din("sink", [2, 16])
    gtb_in = din("gtbT", [2, 128, 48])
    lng_in = din("ln_g", [2, D])
    lnb_in = din("ln_b", [2, D])
    h0s = nc.dram_tensor("h0s", [18 * 128, D], F32, kind="Internal").ap()
    if debug:
        h1_kind = "ExternalOutput"
    h1s = nc.dram_tensor("h1s", [18 * 128, D], F32, kind=h1_kind).ap()
    out = nc.dram_tensor("out", [16 * 128, D], F32, kind="ExternalOutput").ap() if (1 in layers) else None
    h0s_b = [Buf("h0s%d" % i) for i in range(18)]
    h1s_b = [Buf("h1s%d" % i) for i in range(18)]

    es = ExitStack()
    with es:
        def sb(name, shape, dt):
            return es.enter_context(nc.sbuf_tensor(name, list(shape), dt))

        ident = sb("ident", [128, 128], BF16)
        ones_bf = sb("ones_bf", [128, 128], BF16)
        masks = sb("masks", [128, 4, 512], BF16)
        edge = sb("edge", [128, 2], F32)
        invf = sb("invf_s", [128, 16], F32)
        posi = sb("posi", [128, 20], I32)
        posf = sb("posf", [128, 20], F32)
        CC = sb("CC", [128, 20, 32], F32)
        SS = sb("SS", [128, 20, 32], F32)
        cw = sb("cw", [128, 8, 3], F32)
        glg = sb("glg", [128, 8], F32)
        glb = sb("glb", [128, 8], F32)
        gtb = sb("gtb", [128, 48], F32)
        wsT = sb("wsT_s", [128, 8, 128], BF16)
        bsb = sb("bsb", [128, 8, 128], F32)
        Rt = sb("Rt", [128, 8, 128], F32)
        skb = sb("skb", [128, 16], F32)
        esk = sb("esk", [128, 16], F32)
        st6 = sb("st6", [128, 8, 4, 6], F32)
        mv = sb("mv", [128, 8, 2], F32)
        ve = sb("ve", [128, 8], F32)
        sq = sb("sq", [128, 8], F32)
        rstd = sb("rstd", [128, 8], F32)
        nbias = sb("nbias", [128, 8], F32)
        hst6 = sb("hst6", [128, 3, 4, 6], F32)
        hmv = sb("hmv", [128, 3, 2], F32)
        hsm = sb("hsm", [128, 3, 4], F32)
        slots = [sb("slot%d" % i, [128, 4096], BF16) for i in range(NWS)]
        hT = sb("hT", [128, 16, 1024], BF16)
        yT = sb("yT", [128, 32, 768], BF16)
        mT = sb("mT", [128, 16, 768], BF16)
        Fr = sb("Fr", [128, 24576], BF16)
        psum = [es.enter_context(nc.psum_tensor("ps%d" % i, [128, 1024], F32)) for i in range(4)]

        B = {n: Buf(n) for n in ("ident", "ones", "masks", "edge", "invf", "posi", "posf", "CC", "SS", "ang", "a1",
                                 "a2", "tq", "ki", "cw", "glg", "glb", "gtb", "wsT", "bsb", "Rt", "skb", "esk",
                                 "st6", "mv", "ve", "sq", "rstd", "nbias", "hT", "yT", "mT")}
        slot_b = [Buf("slot%d" % i) for i in range(NWS)]
        hs_b = [Buf("hs0"), Buf("hs1"), Buf("hs2")]
        ps_b = [Buf("ps%d" % i) for i in range(4)]
        zb_b = [Buf("zb%d" % i) for i in range(6)]
        F_live = []

        def fview(off, n, dt, **kw):
            ap = Fr[:, off:off + n]
            if dt != BF16:
                ap = ap.bitcast(dt)
            return ap

        def new_phase(names):
            dep = {}
            for b in F_live:
                _merge(dep, b.w)
                _merge(dep, b.r)
            del F_live[:]
            res = {}
            for n in names:
                res[n] = Buf(n, dep)
                F_live.append(res[n])
            return res

        ang = fview(0, 640, F32).rearrange("p (a b) -> p a b", b=16)
        a1 = fview(640, 640, F32).rearrange("p (a b) -> p a b", b=16)
        a2 = fview(1280, 640, F32).rearrange("p (a b) -> p a b", b=16)
        tq = fview(1920, 640, F32).rearrange("p (a b) -> p a b", b=16)
        ki = fview(2560, 640, I32).rearrange("p (a b) -> p a b", b=16)
        for _n in ("ang", "a1", "a2", "tq", "ki"):
            F_live.append(B[_n])

        uidx = [0]

        def next_unit():
            i = uidx[0] % 4
            uidx[0] += 1
            return psum[i], ps_b[i]

        sidx = [0]

        slot_specs = {}
        tile_n = [0]
        cur = {"l": 0, "rec": False}

        def load_slot(specs):
            i = sidx[0] % NWS
            sidx[0] += 1
            n = tile_n[0]
            tile_n[0] += 1
            lst = slot_specs.setdefault(cur["l"], [])
            if cur["rec"]:
                lst.append(list(specs))
            else:
                assert lst[n] == list(specs), "slot sequence differs between tiles"
            src = wstream[cur["l"], n]
            pairs = [(slots[i][:, h * 2048:(h + 1) * 2048], src[:, h * 2048:(h + 1) * 2048]) for h in range(2)]
            P.dma("pool", pairs, "w%d" % i, writes=[slot_b[i]])
            return slots[i], slot_b[i]

        def s3(slot):
            return slot[:, :].rearrange("p (k c) -> p k c", c=128)

        def s2(slot):
            return slot[:, :].rearrange("p (k c) -> p k c", c=256)

        def mm_group(out_ap, pairs, pb, reads):
            n = len(pairs)
            for i, (l, r) in enumerate(pairs):
                P.op("pe", lambda e, l=l, r=r, i=i: e.matmul(out_ap, l, r, start=(i == 0), stop=(i == n - 1)),
                     reads=reads, writes=[pb], inc=(i == n - 1))

        def act(out, in_, func, reads, writes, **kw):
            P.op("act", lambda e: e.activation(out=out, in_=in_, func=func, **kw), reads=reads, writes=writes)

        def dve(fn, reads, writes):
            P.op("dve", fn, reads=reads, writes=writes)

        def tt(out, in0, in1, op, reads, writes):
            dve(lambda e: e.tensor_tensor(out=out, in0=in0, in1=in1, op=op), reads, writes)

        def stt(out, in0, scalar, in1, op0, op1, reads, writes):
            dve(lambda e: e.scalar_tensor_tensor(out=out, in0=in0, scalar=scalar, in1=in1, op0=op0, op1=op1),
                reads, writes)

        def ts(out, in0, s1, s2_, op0, op1, reads, writes):
            if s2_ is None:
                dve(lambda e: e.tensor_scalar(out=out, in0=in0, scalar1=s1, scalar2=None, op0=op0), reads, writes)
            else:
                dve(lambda e: e.tensor_scalar(out=out, in0=in0, scalar1=s1, scalar2=s2_, op0=op0, op1=op1),
                    reads, writes)

        def cp(out, in_, reads, writes):
            dve(lambda e: e.tensor_copy(out=out, in_=in_), reads, writes)

        P.dma("sp", [(edge[:], edge_in), (invf[:], invf_in), (posi[:], posT)], "cst",
              writes=[B["edge"], B["invf"], B["posi"]])
        P.dma("pool", [(ident[:], ident_in), (masks[:], mask_in)], "cstp", writes=[B["ident"], B["masks"]])
        dve(lambda e: e.memset(ones_bf[:], 1.0), [], [B["ones"]])
        cp(posf[:], posi[:], [B["posi"]], [B["posf"]])
        tt(ang[:], posf[:].unsqueeze(2).broadcast_to([128, 20, 16]), invf[:].unsqueeze(1).broadcast_to([128, 20, 16]),
           ALU.mult, [B["posf"], B["invf"]], [B["ang"]])
        for (dst, dn, shift) in ((a1, "a1", 0.0), (a2, "a2", math.pi / 2)):
            ts(dst[:], ang[:], shift, None, ALU.add, None, [B["ang"]], [B[dn]])
            ts(tq[:], dst[:], 1.0 / TWO_PI, None, ALU.mult, None, [B[dn]], [B["tq"]])
            cp(ki[:], tq[:], [B["tq"]], [B["ki"]])
            cp(tq[:], ki[:], [B["ki"]], [B["tq"]])
            stt(dst[:], tq[:], -TWO_PI, dst[:], ALU.mult, ALU.add, [B["tq"], B[dn]], [B[dn]])
            ts(tq[:], dst[:], math.pi, -TWO_PI, ALU.is_gt, ALU.mult, [B[dn]], [B["tq"]])
            tt(dst[:], dst[:], tq[:], ALU.add, [B[dn], B["tq"]], [B[dn]])
            ts(dst[:], dst[:], -PI_SAFE, PI_SAFE, ALU.max, ALU.min, [B[dn]], [B[dn]])
        act(SS[:, :, 0:16], a1[:], AF.Sin, [B["a1"]], [B["SS"]], scale=-1.0)
        act(SS[:, :, 16:32], a1[:], AF.Sin, [B["a1"]], [B["SS"]])
        act(CC[:, :, 0:16], a2[:], AF.Sin, [B["a2"]], [B["CC"]])
        act(CC[:, :, 16:32], a2[:], AF.Sin, [B["a2"]], [B["CC"]])

        store_keys = set()
        taps = {}
        cur_layer = [0]

        def tap(name, ap2d, bufs):
            if not debug or name in taps or cur_layer[0] != dbg_layer:
                return
            t = nc.dram_tensor("dbg_" + name, [128, ap2d.shape[1]], ap2d.dtype, kind="ExternalOutput").ap()
            taps[name] = t
            P.dma("sp", [(t, ap2d)], "dbg", reads=bufs)
            store_keys.add("dbg")

        for l in layers:
            first = (l == 0)
            cur_layer[0] = l
            src = xs if first else h1s
            tiles = [(1, 6), (7, 6), (13, 6)] if first else [(1, 6), (7, 6), (13, 4)]
            pboff = 0 if first else 1
            tl_edge = 255 if first else 127
            th_edge = 2304 if first else 2176
            cur["l"] = l
            P.dma("sp", [(cw[:], convw_in[l]), (glg[:], glg_in[l]), (glb[:], glb_in[l]), (gtb[:], gtb_in[l]),
                         (bsb[:].rearrange("p a b -> p (a b)"), bs_in[l].partition_broadcast(128)),
                         (skb[:], sink_in[l].partition_broadcast(128))], "par%d" % l,
                  writes=[B["cw"], B["glg"], B["glb"], B["gtb"], B["bsb"], B["skb"]])
            P.dma("pool", [(wsT[:], ws_in[l])], "parp%d" % l, writes=[B["wsT"]])
            act(esk[:], skb[:], AF.Exp, [B["skb"]], [B["esk"]])
            ps, pb = next_unit()
            for g in range(8):
                P.op("pe", lambda e, g=g, ps=ps: e.matmul(ps[:, g * 128:(g + 1) * 128], ones_bf[:], wsT[:, g, :],
                                                          start=True, stop=True),
                     reads=[B["ones"], B["wsT"]], writes=[pb], inc=(g == 7))
            for g in range(8):
                stt(Rt[:, g, :], ps[:, g * 128:(g + 1) * 128], glb[:, g:g + 1], bsb[:, g, :], ALU.mult, ALU.add,
                    [pb, B["glb"], B["bsb"]], [B["Rt"]])

            for ti, (a, nb) in enumerate(tiles):
                T = nb * 128
                tile_n[0] = 0
                cur["rec"] = (ti == 0)
                nh = nb + 2
                if nb == 6:
                    csub = [(128, 384), (512, 384)]
                    xsub = [(127, 385), (512, 385)]
                else:
                    csub = [(128, 512)]
                    xsub = [(127, 257), (384, 257)]
                nsub = len(csub)

                def psv(ps, subs):
                    n = subs[0][1]
                    return ps[:, :].rearrange("p (j n) -> p j n", n=512)[:, 0:len(subs), 0:n]

                def sbv(ap, subs):
                    n = subs[0][1]
                    return ap.rearrange("p (j n) -> p j n", n=n)

                def fm_unit(slot, sbuf_, blk0, nk, rhs_fn, subs, reads):
                    ps, pb = next_unit()
                    s = s3(slot)
                    for j, (t0, n) in enumerate(subs):
                        pairs = [(s[:, blk0 + kc, :], rhs_fn(kc, t0, n)) for kc in range(nk)]
                        mm_group(ps[:, j * 512:j * 512 + n], pairs, pb, reads + [sbuf_])
                    return ps, pb

                def hrhs(kc, t0, n):
                    return hT[:, kc, t0:t0 + n]

                def wchunk(c0, blk0):
                    return [(128, blk0, 0, 128, "w_in", c0, 16)]

                Fb = new_phase(["xb0", "xb1", "xb2", "xbf0", "xbf1", "l0g", "l0b"])
                xbs = [fview(0, 4096, F32), fview(4096, 4096, F32), fview(8192, 4096, F32)]
                xbfs = [fview(12288, 2048, BF16), fview(14336, 2048, BF16)]
                l0g = fview(16384, 4096, F32)
                l0b = fview(20480, 4096, F32)
                if first:
                    P.dma("sp", [(l0g, ln0g_in.partition_broadcast(128)), (l0b, ln0b_in.partition_broadcast(128))],
                          "l0p", writes=[Fb["l0g"], Fb["l0b"]])
                def head_vars(hb):
                    par = hb % 3
                    return (a - 1 + hb, par, xbs[par], Fb["xb%d" % par], xbfs[hb % 2], Fb["xbf%d" % (hb % 2)], hs_b[par])

                def head_stage1(hb):
                    sblk, par, xb, xbB, xbf, xbfB, hsB = head_vars(hb)
                    rd = [] if first else [h1s_b[sblk]]
                    P.dma("sp", [(xb, src[sblk * 128:(sblk + 1) * 128, :])], "xl%d" % par, reads=rd, writes=[xbB])
                    if first:
                        for j in range(4):
                            dve(lambda e, j=j, xb=xb, par=par: e.bn_stats(out=hst6[:, par, j, :],
                                                                          in_=xb[:, j * 512:(j + 1) * 512]),
                                [xbB], [hsB])
                        dve(lambda e, par=par: e.bn_aggr(out=hmv[:, par, :], in_=hst6[:, par, :, :]), [hsB], [hsB])
                        ts(hsm[:, par, 0:1], hmv[:, par, 1:2], LN_EPS, None, ALU.add, None, [hsB], [hsB])
                        act(hsm[:, par, 1:2], hsm[:, par, 0:1], AF.Sqrt, [hsB], [hsB])
                        dve(lambda e, par=par: e.reciprocal(out=hsm[:, par, 2:3], in_=hsm[:, par, 1:2]), [hsB], [hsB])
                        stt(hsm[:, par, 3:4], hmv[:, par, 0:1], -1.0, hsm[:, par, 2:3], ALU.mult, ALU.mult, [hsB], [hsB])

                def head_stage2a(hb):
                    sblk, par, xb, xbB, xbf, xbfB, hsB = head_vars(hb)
                    if first:
                        act(xb, xb, AF.Identity, [xbB, hsB], [xbB], scale=hsm[:, par, 2:3], bias=hsm[:, par, 3:4])
                        tt(xb, xb, l0g, ALU.mult, [xbB, Fb["l0g"]], [xbB])
                        tt(xb, xb, l0b, ALU.add, [xbB, Fb["l0b"]], [xbB])
                        if 1 <= hb <= nb:
                            P.dma("sp", [(h0s[(sblk - 1) * 128:sblk * 128, :], xb)], "sh%d" % par, reads=[xbB],
                                  writes=[h0s_b[sblk - 1]])
                            store_keys.add("sh%d" % par)
                        act(xbf, xb, AF.Identity, [xbB], [xbfB])
                    else:
                        cp(xbf, xb, [xbB], [xbfB])

                def head_stage2b(hb):
                    sblk, par, xb, xbB, xbf, xbfB, hsB = head_vars(hb)
                    ps, pb = next_unit()
                    psb = ps[:, :].bitcast(BF16).rearrange("p (k c) -> p k c", c=128)
                    for kc in range(16):
                        P.op("pe", lambda e, kc=kc, psb=psb, xbf=xbf: e.transpose(psb[:, kc, :],
                                                                                  xbf[:, kc * 128:(kc + 1) * 128], ident[:]),
                             reads=[xbfB, B["ident"]], writes=[pb], inc=(kc == 15))
                    act(hT[:, :, hb * 128:(hb + 1) * 128], psb[:, 0:16, :], AF.Identity, [pb], [B["hT"]])

                head_stage1(0)
                if nh > 1:
                    head_stage1(1)
                head_stage2a(0)
                for hb in range(nh):
                    if hb + 2 < nh:
                        head_stage1(hb + 2)
                    if hb + 1 < nh:
                        head_stage2a(hb + 1)
                    head_stage2b(hb)

                tap("hT", hT[:, :, :].rearrange("p a b -> p (a b)"), [B["hT"]])
                Fb = new_phase(["ac0", "yy0", "cv0", "sz0", "t10", "ac1", "yy1", "cv1", "sz1", "t11"])
                for g in range(8):
                    pr = g % 2
                    base = pr * 8192
                    ac = fview(base, 1600, F32)[:, 0:T + 2]
                    yy = fview(base + 1600, 1600, F32)[:, 0:T + 2]
                    cv = fview(base + 3200, 1536, F32)[:, 0:T]
                    sz = fview(base + 4800, 1536, F32)[:, 0:T]
                    t1 = fview(base + 6400, 1536, F32)[:, 0:T]
                    bac, byy, bcv, bsz, bt1 = (Fb["ac%d" % pr], Fb["yy%d" % pr], Fb["cv%d" % pr], Fb["sz%d" % pr],
                                               Fb["t1%d" % pr])
                    c = g * 128
                    s1, s1b = load_slot(wchunk(OFF["ac"] + c, 0) + wchunk(OFF["ah"] + c, 16))
                    s2_, s2b = load_slot(wchunk(OFF["ab"] + c, 0) + wchunk(OFF["az"] + c, 16))
                    pac, pacb = fm_unit(s1, s1b, 0, 16, hrhs, xsub, [B["hT"]])
                    act(sbv(ac, xsub), psv(pac, xsub), AF.Identity, [pacb], [bac])
                    pah, pahb = fm_unit(s1, s1b, 16, 16, hrhs, xsub, [B["hT"]])
                    tt(sbv(yy, xsub), psv(pah, xsub), sbv(ac, xsub), ALU.mult, [pahb, bac], [byy])
                    t_lo = (a - 1) * 128 + 127
                    for (te, ecol) in ((tl_edge, 0), (th_edge, 1)):
                        j = te - t_lo
                        if 0 <= j < T + 2:
                            ts(yy[:, j:j + 1], yy[:, j:j + 1], edge[:, ecol:ecol + 1], None, ALU.mult, None,
                               [byy, B["edge"]], [byy])
                    ts(cv, yy[:, 0:T], cw[:, g, 0:1], None, ALU.mult, None, [byy, B["cw"]], [bcv])
                    stt(cv, yy[:, 1:T + 1], cw[:, g, 1:2], cv, ALU.mult, ALU.add, [byy, B["cw"], bcv], [bcv])
                    stt(cv, yy[:, 2:T + 2], cw[:, g, 2:3], cv, ALU.mult, ALU.add, [byy, B["cw"], bcv], [bcv])
                    pab, pabb = fm_unit(s2_, s2b, 0, 16, hrhs, csub, [B["hT"]])
                    tt(sbv(t1, csub), psv(pab, csub), sbv(cv, csub), ALU.mult, [pabb, bcv], [bt1])
                    paz, pazb = fm_unit(s2_, s2b, 16, 16, hrhs, csub, [B["hT"]])
                    act(sbv(sz, csub), psv(paz, csub), AF.Silu, [pazb], [bsz])
                    tt(yT[:, g, 0:T], t1, sz, ALU.mult, [bt1, bsz], [B["yT"]])

                Fb = new_phase(["gv", "nrm", "u0", "z0", "m0", "u1", "z1", "m1"])
                gv = fview(0, 12288, F32).rearrange("p (b c) -> p b c", c=1024)
                nrm = fview(12288, 6144, BF16).rearrange("p (b c) -> p b c", c=1024)
                for s in range(4):
                    c0 = OFF["gv"] + s * 256
                    sl, slb = load_slot([(256, 0, 0, 256, "w_in", c0, 16)])
                    for b0 in range(0, nb, 4):
                        nbu = min(4, nb - b0)
                        ps, pb = next_unit()
                        for i in range(nbu):
                            tb = 128 + (b0 + i) * 128
                            pairs = [(hT[:, kc, tb:tb + 128], s2(sl)[:, kc, :]) for kc in range(16)]
                            mm_group(ps[:, i * 256:(i + 1) * 256], pairs, pb, [B["hT"], slb])
                        act(gv[:, b0:b0 + nbu, s * 256:(s + 1) * 256],
                            ps[:, 0:nbu * 256].rearrange("p (b c) -> p b c", c=256), AF.Gelu_apprx_tanh, [pb], [Fb["gv"]])
                def g_bufs(g):
                    pr = g % 2
                    u = fview(18432 + pr * 3072, 1536, F32)[:, 0:T]
                    z = fview(18432 + pr * 3072 + 1536, 1536, F32)[:, 0:T]
                    m = fview(pr * 1536, 1536, F32)[:, 0:T]
                    return u, z, m, Fb["u%d" % pr], Fb["z%d" % pr], Fb["m%d" % pr]

                def uz(g):
                    u, z, m, bu, bz, bm = g_bufs(g)
                    c = g * 128
                    sl, slb = load_slot(wchunk(OFF["gu"] + c, 0) + wchunk(OFF["gz"] + c, 16))
                    pu, pub = fm_unit(sl, slb, 0, 16, hrhs, csub, [B["hT"]])
                    act(sbv(u, csub), psv(pu, csub), AF.Gelu_apprx_tanh, [pub], [bu])
                    pz, pzb = fm_unit(sl, slb, 16, 16, hrhs, csub, [B["hT"]])
                    act(sbv(z, csub), psv(pz, csub), AF.Silu, [pzb], [bz])

                def spatial(g):
                    u, z, m, bu, bz, bm = g_bufs(g)
                    psp, pspb = next_unit()
                    for b in range(nb):
                        P.op("pe", lambda e, b=b, g=g, psp=psp: e.matmul(psp[:, b * 128:(b + 1) * 128],
                                                                         nrm[:, b, g * 128:(g + 1) * 128], wsT[:, g, :],
                                                                         start=True, stop=True),
                             reads=[Fb["nrm"], B["wsT"]], writes=[pspb], inc=(b == nb - 1))
                    stt(m.rearrange("p (b c) -> p b c", c=128), psp[:, 0:T].rearrange("p (b c) -> p b c", c=128),
                        glg[:, g:g + 1], Rt[:, g, :].unsqueeze(1).broadcast_to([128, nb, 128]), ALU.mult, ALU.add,
                        [pspb, B["glg"], B["Rt"], Fb["gv"], Fb["nrm"]], [bm, Fb["gv"]])
                    tt(m, m, u, ALU.mult, [bm, bu], [bm])
                    tt(yT[:, 8 + g, 0:T], m, z, ALU.mult, [bm, bz], [B["yT"]])

                uz(0)
                for b in range(nb):
                    for j in range(2):
                        dve(lambda e, b=b, j=j: e.bn_stats(out=st6[:, b, j, :], in_=gv[:, b, j * 512:(j + 1) * 512]),
                            [Fb["gv"]], [B["st6"]])
                    dve(lambda e, b=b: e.bn_aggr(out=mv[:, b, :], in_=st6[:, b, 0:2, :]), [B["st6"]], [B["mv"]])
                ts(ve[:, 0:nb], mv[:, 0:nb, 1], LN_EPS, None, ALU.add, None, [B["mv"]], [B["ve"]])
                act(sq[:, 0:nb], ve[:, 0:nb], AF.Sqrt, [B["ve"]], [B["sq"]])
                dve(lambda e, nb=nb: e.reciprocal(out=rstd[:, 0:nb], in_=sq[:, 0:nb]), [B["sq"]], [B["rstd"]])
                stt(nbias[:, 0:nb], mv[:, 0:nb, 0], -1.0, rstd[:, 0:nb], ALU.mult, ALU.mult, [B["mv"], B["rstd"]],
                    [B["nbias"]])
                for b in range(nb):
                    act(nrm[:, b, :], gv[:, b, :], AF.Identity, [Fb["gv"], B["rstd"], B["nbias"]], [Fb["nrm"]],
                        scale=rstd[:, b:b + 1], bias=nbias[:, b:b + 1])
                for g in range(8):
                    if g + 1 < 8:
                        uz(g + 1)
                    spatial(g)

                Fb = new_phase(["kvf0", "kvf1", "kbf", "kT", "qf", "qbf", "qT", "szc", "P0", "P1", "dsb", "o", "tr1",
                                "tr2"])
                kvfs = [fview(0, 2048, F32).rearrange("p (b c) -> p b c", c=256),
                        fview(2048, 2048, F32).rearrange("p (b c) -> p b c", c=256)]
                kbf = fview(4096, 2048, BF16).rearrange("p (b c) -> p b c", c=256)
                kT = fview(6144, 1024, BF16)
                qf = fview(7168, 2048, F32).rearrange("p (b c) -> p b c", c=512)
                qbf = fview(9216, 3072, BF16).rearrange("p (b c) -> p b c", c=512)
                qT = fview(12288, 3072, BF16)
                szc = fview(15360, 3072, BF16)
                Pt = [fview(18432, 1536, BF16), fview(19968, 1536, BF16)]
                dsb = fview(21504, 1024, F32)
                ot = fview(22528, 1024, F32)
                tr1 = fview(23552, 512, F32)
                tr2 = fview(24064, 512, F32)
                for kvh in range(4):
                    ck = OFF["k"] + kvh * 128
                    cvv = OFF["v"] + kvh * 128
                    sl, slb = load_slot([(256, 0, 0, 128, "w_in", ck, 16), (256, 0, 128, 128, "w_in", cvv, 16)])
                    for ui, b0 in enumerate(range(0, nh, 4)):
                        nbu = min(4, nh - b0)
                        kvf = kvfs[ui % 2]
                        bkvf = Fb["kvf%d" % (ui % 2)]
                        ps, pb = next_unit()
                        for i in range(nbu):
                            tb = (b0 + i) * 128
                            pairs = [(hT[:, kc, tb:tb + 128], s2(sl)[:, kc, :]) for kc in range(16)]
                            mm_group(ps[:, i * 256:(i + 1) * 256], pairs, pb, [B["hT"], slb])
                        act(kvf[:, 0:nbu, :], ps[:, 0:nbu * 256].rearrange("p (b c) -> p b c", c=256), AF.Identity, [pb],
                            [bkvf])
                        cp(kbf[:, b0:b0 + nbu, 32:256], kvf[:, 0:nbu, 32:256], [bkvf], [Fb["kbf"]])
                        tb0 = a - 1 + b0 + pboff
                        r1 = tr1[:, 0:nbu * 32].rearrange("p (b c) -> p b c", c=32)
                        r2 = tr2[:, 0:nbu * 32].rearrange("p (b c) -> p b c", c=32)
                        tt(r1, kvf[:, 0:nbu, 0:32], CC[:, tb0:tb0 + nbu, :], ALU.mult, [bkvf, B["CC"]], [Fb["tr1"]])
                        tt(r2[:, :, 0:16], kvf[:, 0:nbu, 16:32], SS[:, tb0:tb0 + nbu, 0:16], ALU.mult, [bkvf, B["SS"]],
                           [Fb["tr2"]])
                        tt(r2[:, :, 16:32], kvf[:, 0:nbu, 0:16], SS[:, tb0:tb0 + nbu, 16:32], ALU.mult, [bkvf, B["SS"]],
                           [Fb["tr2"]])
                        tt(kbf[:, b0:b0 + nbu, 0:32], r1, r2, ALU.add, [Fb["tr1"], Fb["tr2"]], [Fb["kbf"]])
                    qs = []
                    for s in range(2):
                        c0 = OFF["q"] + kvh * 512 + s * 256
                        qs.append(load_slot([(256, 0, 0, 256, "w_in", c0, 16)]))
                    for b0 in range(0, nb, 2):
                        ps, pb = next_unit()
                        for i in range(2):
                            tb = 128 + (b0 + i) * 128
                            for s in range(2):
                                pairs = [(hT[:, kc, tb:tb + 128], s2(qs[s][0])[:, kc, :]) for kc in range(16)]
                                mm_group(ps[:, i * 512 + s * 256:i * 512 + (s + 1) * 256], pairs, pb,
                                         [B["hT"], qs[s][1]])
                        act(qf, ps[:, :].rearrange("p (b c) -> p b c", c=512), AF.Identity, [pb], [Fb["qf"]])
                        qf4 = qf.rearrange("p b (h d) -> p b h d", d=128)
                        qb4 = qbf[:, b0:b0 + 2, :].rearrange("p b (h d) -> p b h d", d=128)
                        cp(qb4[:, :, :, 32:128], qf4[:, :, :, 32:128], [Fb["qf"]], [Fb["qbf"]])
                        tb0 = a + b0 + pboff
                        r1 = tr1[:, 0:256].rearrange("p (b h c) -> p b h c", h=4, c=32)
                        r2 = tr2[:, 0:256].rearrange("p (b h c) -> p b h c", h=4, c=32)
                        CCb = CC[:, tb0:tb0 + 2, :].unsqueeze(2).broadcast_to([128, 2, 4, 32])
                        SSb = SS[:, tb0:tb0 + 2, :].unsqueeze(2).broadcast_to([128, 2, 4, 32])
                        tt(r1, qf4[:, :, :, 0:32], CCb, ALU.mult, [Fb["qf"], B["CC"]], [Fb["tr1"]])
                        tt(r2[:, :, :, 0:16], qf4[:, :, :, 16:32], SSb[:, :, :, 0:16], ALU.mult, [Fb["qf"], B["SS"]],
                           [Fb["tr2"]])
                        tt(r2[:, :, :, 16:32], qf4[:, :, :, 0:16], SSb[:, :, :, 16:32], ALU.mult, [Fb["qf"], B["SS"]],
                           [Fb["tr2"]])
                        tt(qb4[:, :, :, 0:32], r1, r2, ALU.add, [Fb["tr1"], Fb["tr2"]], [Fb["qbf"]])
                    nbs = nb // nsub
                    szc5 = szc[:, 0:nb * 512].rearrange("p (j b h q) -> p j b h q", j=nsub, b=nbs, h=4)
                    for s in range(2):
                        c0 = OFF["cz"] + kvh * 512 + s * 256
                        sl, slb = load_slot(wchunk(c0, 0) + wchunk(c0 + 128, 16))
                        for hh in range(2):
                            h = s * 2 + hh
                            pc, pcb = fm_unit(sl, slb, hh * 16, 16, hrhs, csub, [B["hT"]])
                            act(szc5[:, :, :, h, :], psv(pc, csub).rearrange("p j (b q) -> p j b q", q=128), AF.Silu,
                                [pcb], [Fb["szc"]])
                    ps, pb = next_unit()
                    psb = ps[:, :].bitcast(BF16).rearrange("p (k c) -> p k c", c=128)
                    for hb in range(nh):
                        P.op("pe", lambda e, hb=hb, psb=psb: e.transpose(psb[:, hb, :], kbf[:, hb, 0:128], ident[:]),
                             reads=[Fb["kbf"], B["ident"]], writes=[pb], inc=(hb == nh - 1))
                    cp(kT[:, 0:nh * 128], ps[:, :].bitcast(BF16)[:, 0:nh * 128], [pb], [Fb["kT"]])
                    for b0 in range(0, nb, 4):
                        nbu = min(4, nb - b0)
                        ps, pb = next_unit()
                        psb = ps[:, :].bitcast(BF16).rearrange("p (k c) -> p k c", c=128)
                        for i in range(nbu):
                            for h in range(4):
                                last = (i == nbu - 1 and h == 3)
                                P.op("pe", lambda e, i=i, h=h, b0=b0, psb=psb: e.transpose(
                                    psb[:, i * 4 + h, :], qbf[:, b0 + i, h * 128:(h + 1) * 128], ident[:]),
                                     reads=[Fb["qbf"], B["ident"]], writes=[pb], inc=last)
                        cp(qT[:, b0 * 512:(b0 + nbu) * 512], ps[:, :].bitcast(BF16)[:, 0:nbu * 512], [pb], [Fb["qT"]])
                    pend = {}
                    for it in range(nb + 1):
                        if it < nb:
                            b = it
                            xloc = a + b + pboff
                            mprev = masks[:, 2, :] if xloc == 2 else masks[:, 0, :]
                            mnext = masks[:, 3, :] if xloc == 17 else masks[:, 1, :]
                            Pb_ = Pt[b % 2]
                            bP = Fb["P%d" % (b % 2)]
                            qrhs = qT[:, b * 512:(b + 1) * 512]
                            psA, pbA = next_unit()
                            mm_group(psA[:, 0:512], [(kT[:, b * 128:(b + 1) * 128], qrhs), (ident[:], mprev)], pbA,
                                     [Fb["kT"], Fb["qT"], B["ident"], B["masks"]])
                            mm_group(psA[:, 512:1024], [(kT[:, (b + 1) * 128:(b + 2) * 128], qrhs)], pbA,
                                     [Fb["kT"], Fb["qT"]])
                            act(Pb_[:, 0:1024], psA[:, :], AF.Exp, [pbA], [bP], scale=SCALE)
                            psB, pbB = next_unit()
                            mm_group(psB[:, 0:512], [(kT[:, (b + 2) * 128:(b + 3) * 128], qrhs), (ident[:], mnext)], pbB,
                                     [Fb["kT"], Fb["qT"], B["ident"], B["masks"]])
                            act(Pb_[:, 1024:1536], psB[:, 0:512], AF.Exp, [pbB], [bP], scale=SCALE)
                        if it >= 1:
                            b = it - 1
                            Pb_ = Pt[b % 2]
                            bP = Fb["P%d" % (b % 2)]
                            ps, pb = next_unit()
                            mm_group(ps[:, 0:512], [(kbf[:, b + j, 128:256], Pb_[:, j * 512:(j + 1) * 512]) for j in range(3)],
                                     pb, [Fb["kbf"], bP])
                            mm_group(ps[:, 512:1024], [(ones_bf[:], Pb_[:, j * 512:(j + 1) * 512]) for j in range(3)],
                                     pb, [B["ones"], bP])
                            tt(ot.rearrange("p (h q) -> p h q", q=128), ps[:, 0:512].rearrange("p (h q) -> p h q", q=128),
                               szc[:, b * 512:(b + 1) * 512].rearrange("p (h q) -> p h q", q=128), ALU.mult,
                               [pb, Fb["szc"]], [Fb["o"]])
                            tt(dsb.rearrange("p (h q) -> p h q", q=128), ps[:, 512:1024].rearrange("p (h q) -> p h q", q=128),
                               esk[:, kvh * 4:kvh * 4 + 4].unsqueeze(2).broadcast_to([128, 4, 128]), ALU.add,
                               [pb, B["esk"]], [Fb["dsb"]])
                            dve(lambda e: e.reciprocal(out=dsb, in_=dsb), [Fb["dsb"]], [Fb["dsb"]])
                            tt(yT[:, 16 + kvh * 4:16 + kvh * 4 + 4, b * 128:(b + 1) * 128],
                               ot.rearrange("p (h q) -> p h q", q=128), dsb.rearrange("p (h q) -> p h q", q=128), ALU.mult,
                               [Fb["o"], Fb["dsb"]], [B["yT"]])

                tap("yT", yT[:, :, :].rearrange("p a b -> p (a b)"), [B["yT"]])
                Fb = new_phase(["sg0", "sg1", "sg2", "sg3", "sg4", "sg5", "mm", "tm"])
                sgs = [fview(i * 3072, 3072, F32)[:, 0:T] for i in range(6)]
                mmt = fview(18432, 3072, F32)[:, 0:T]
                tmt = fview(21504, 3072, F32)[:, 0:T]

                def gate_slot(c0a, c0b):
                    return load_slot(wchunk(c0a, 0) + wchunk(c0b, 16))

                def gate_unit(sl, slb, blk0, si, col):
                    pg, pgb = fm_unit(sl, slb, blk0, 16, hrhs, csub, [B["hT"]])
                    act(sbv(sgs[si], csub), psv(pg, csub), AF.Sigmoid, [pgb, B["gtb"]], [Fb["sg%d" % si]],
                        bias=gtb[:, col:col + 1])

                def branch(j, sis):
                    c = j * 128
                    sl, slb = load_slot([(128, 0, 0, 128, "w_branch_a", c, 8), (128, 8, 0, 128, "w_branch_b", c, 8),
                                         (128, 16, 0, 128, "w_branch_c", c, 16)])
                    first_ = True
                    for (blk0, nk, y0, si) in ((0, 8, 0, sis[0]), (8, 8, 8, sis[1]), (16, 16, 16, sis[2])):
                        pp, ppb = fm_unit(sl, slb, blk0, nk, lambda kc, t0, n, y0=y0: yT[:, y0 + kc, t0 - 128:t0 - 128 + n],
                                          csub, [B["yT"]])
                        dst = mmt if first_ else tmt
                        dstb = Fb["mm"] if first_ else Fb["tm"]
                        tt(sbv(dst, csub), psv(pp, csub), sbv(sgs[si], csub), ALU.mult, [ppb, Fb["sg%d" % si]], [dstb])
                        if not first_:
                            if blk0 == 16:
                                tt(mT[:, j, 0:T], mmt, tmt, ALU.add, [Fb["mm"], Fb["tm"]], [B["mT"]])
                            else:
                                tt(mmt, mmt, tmt, ALU.add, [Fb["mm"], Fb["tm"]], [Fb["mm"]])
                        first_ = False

                for j0 in range(0, 16, 2):
                    ca, cb_, cc = OFF["ra"], OFF["rb"], OFF["rc"]
                    g0 = gate_slot(ca + j0 * 128, cb_ + j0 * 128)
                    gate_unit(g0[0], g0[1], 0, 0, 0 * 16 + j0)
                    gate_unit(g0[0], g0[1], 16, 1, 1 * 16 + j0)
                    g1 = gate_slot(cc + j0 * 128, ca + (j0 + 1) * 128)
                    gate_unit(g1[0], g1[1], 0, 2, 2 * 16 + j0)
                    gate_unit(g1[0], g1[1], 16, 3, 0 * 16 + j0 + 1)
                    branch(j0, (0, 1, 2))
                    g2 = gate_slot(cb_ + (j0 + 1) * 128, cc + (j0 + 1) * 128)
                    gate_unit(g2[0], g2[1], 0, 4, 1 * 16 + j0 + 1)
                    gate_unit(g2[0], g2[1], 16, 5, 2 * 16 + j0 + 1)
                    branch(j0 + 1, (3, 4, 5))

                tap("mT", mT[:, :, :].rearrange("p a b -> p (a b)"), [B["mT"]])
                Fb = new_phase(["rb0", "rb1", "lg", "lb"])
                rbs = [fview(0, 4096, F32), fview(4096, 4096, F32)]
                lgt = fview(8192, 4096, F32)
                lbt = fview(12288, 4096, F32)
                P.dma("sp", [(lgt, lng_in[l].partition_broadcast(128)), (lbt, lnb_in[l].partition_broadcast(128))],
                      "lnp", writes=[Fb["lg"], Fb["lb"]])
                zbuf = yT[:, :, :].rearrange("p a b -> p (a b)").bitcast(F32).rearrange("p (b c) -> p b c", c=D)
                def t_pass(groups_):
                    for s in range(8):
                        c0 = s * 256
                        sl, slb = load_slot([(256, 0, 0, 256, "w_out", c0, 16)])
                        for (b0, nbu) in groups_:
                            ps, pb = next_unit()
                            for i in range(nbu):
                                tb = (b0 + i) * 128
                                pairs = [(mT[:, kc, tb:tb + 128], s2(sl)[:, kc, :]) for kc in range(16)]
                                mm_group(ps[:, i * 256:(i + 1) * 256], pairs, pb, [B["mT"], slb])
                            act(zbuf[:, b0:b0 + nbu, c0:c0 + 256], ps[:, 0:nbu * 256].rearrange("p (b c) -> p b c", c=256),
                                AF.Identity, [pb], [B["yT"]] + zb_b[b0:b0 + nbu])

                def t_partA(b0, nbu):
                    for b in range(b0, b0 + nbu):
                        sblk = a + b
                        rb = rbs[b % 2]
                        rbB = Fb["rb%d" % (b % 2)]
                        if first:
                            P.dma("sp", [(rb, h0s[(sblk - 1) * 128:sblk * 128, :])], "rs%d" % (b % 2),
                                  reads=[h0s_b[sblk - 1]], writes=[rbB])
                        else:
                            P.dma("sp", [(rb, h1s[sblk * 128:(sblk + 1) * 128, :])], "rs%d" % (b % 2),
                                  reads=[h1s_b[sblk]], writes=[rbB])
                        zb = zbuf[:, b, :]
                        stt(zb, rb, ALPHA, zb, ALU.mult, ALU.add, [rbB, zb_b[b]], [zb_b[b]])
                        for j in range(4):
                            dve(lambda e, b=b, j=j, zb=zb: e.bn_stats(out=st6[:, b, j, :], in_=zb[:, j * 512:(j + 1) * 512]),
                                [zb_b[b]], [B["st6"]])
                        dve(lambda e, b=b: e.bn_aggr(out=mv[:, b, :], in_=st6[:, b, :, :]), [B["st6"]], [B["mv"]])

                def t_rstd(b0, nbu):
                    b1 = b0 + nbu
                    ts(ve[:, b0:b1], mv[:, b0:b1, 1], LN_EPS, None, ALU.add, None, [B["mv"]], [B["ve"]])
                    act(sq[:, b0:b1], ve[:, b0:b1], AF.Sqrt, [B["ve"]], [B["sq"]])
                    dve(lambda e, b0=b0, b1=b1: e.reciprocal(out=rstd[:, b0:b1], in_=sq[:, b0:b1]), [B["sq"]], [B["rstd"]])
                    stt(nbias[:, b0:b1], mv[:, b0:b1, 0], -1.0, rstd[:, b0:b1], ALU.mult, ALU.mult, [B["mv"], B["rstd"]],
                        [B["nbias"]])

                def t_partB(b0, nbu):
                    for b in range(b0, b0 + nbu):
                        sblk = a + b
                        zb = zbuf[:, b, :]
                        act(zb, zb, AF.Identity, [zb_b[b], B["rstd"], B["nbias"]], [zb_b[b]], scale=rstd[:, b:b + 1],
                            bias=nbias[:, b:b + 1])
                        tt(zb, zb, lgt, ALU.mult, [zb_b[b], Fb["lg"]], [zb_b[b]])
                        tt(zb, zb, lbt, ALU.add, [zb_b[b], Fb["lb"]], [zb_b[b]])
                        key = "so%d" % b
                        store_keys.add(key)
                        if first:
                            P.dma("sp", [(h1s[(sblk - 1) * 128:sblk * 128, :], zb)], key, reads=[zb_b[b]],
                                  writes=[h1s_b[sblk - 1]])
                        else:
                            P.dma("sp", [(out[(sblk - 1) * 128:sblk * 128, :], zb)], key, reads=[zb_b[b]])

                groups = [(0, 4), (4, 2)] if nb == 6 else [(0, 4)]
                t_pass(groups)
                t_partA(0, nb)
                t_rstd(0, nb)
                t_partB(0, nb)
                for b in range(nb):
                    _merge(B["yT"].r, zb_b[b].r)
                    _merge(B["yT"].r, zb_b[b].w)

        P.final_wait("sp", sorted(store_keys))

        sems = {}
        for k in P.all_keys():
            sems[k] = es.enter_context(nc.semaphore("s_" + k))
        block = es.enter_context(nc.Block())
        P.replay(block, sems)
    for l in layers:
        assert len(slot_specs[l]) == NSLOT, len(slot_specs[l])
    return nc, slot_specs


def _pack_weights(inputs, specs_by_layer):
    ws = np.zeros((2, NSLOT, 128, 4096), np.float32)
    srcs = {k: np.asarray(inputs[k], np.float32) for k in ("w_in", "w_branch_a", "w_branch_b", "w_branch_c", "w_out")}
    for l, specs in specs_by_layer.items():
        for n, sp in enumerate(specs):
            img = ws[l, n]
            for (V, k0, j0, w, src, c0, nk) in sp:
                v = img.reshape(128, 4096 // V, V)
                W = srcs[src][l]
                v[:, k0:k0 + nk, j0:j0 + w] = W[0:nk * 128, c0:c0 + w].reshape(nk, 128, w).transpose(1, 0, 2)
    return ws


def _host_inputs(inputs, specs_by_layer):
    f32 = np.float32
    x = np.asarray(inputs["x"], dtype=f32).reshape(SEQ, D)
    pos = np.asarray(inputs["positions"]).reshape(SEQ).astype(np.int32)
    xp = np.zeros((SEQ + 4 * 128, D), f32)
    xp[256:256 + SEQ] = x
    pp = np.zeros((SEQ + 4 * 128,), np.int32)
    pp[256:256 + SEQ] = pos
    invf = (500000.0 ** (-(np.arange(16, dtype=f32) / f32(16.0)))).astype(f32)
    invf = np.ascontiguousarray(np.broadcast_to(invf, (128, 16)))
    kk = np.arange(128)[:, None]
    qq = np.arange(128)[None, :]
    tri_prev = np.where(kk >= qq, 0.0, NEG).astype(f32)
    tri_next = np.where(kk <= qq, 0.0, NEG).astype(f32)
    allneg = np.full((128, 128), NEG, f32)
    ident = np.eye(128, dtype=f32)
    shared = dict(
        invf=invf, identin=ident,
        ln0_g=np.asarray(inputs["ln0_g"], f32), ln0_b=np.asarray(inputs["ln0_b"], f32),
        wstream=_pack_weights(inputs, specs_by_layer),
        conv_wT=np.ascontiguousarray(np.asarray(inputs["conv_w"], f32).reshape(2, 3, 8, 128).transpose(0, 3, 2, 1)),
        glgT=np.ascontiguousarray(np.asarray(inputs["gmlp_ln_g"], f32).reshape(2, 8, 128).transpose(0, 2, 1)),
        glbT=np.ascontiguousarray(np.asarray(inputs["gmlp_ln_b"], f32).reshape(2, 8, 128).transpose(0, 2, 1)),
        wsT=np.ascontiguousarray(np.asarray(inputs["spatial_w"], f32).transpose(0, 3, 1, 2)),
        spatial_b=np.ascontiguousarray(np.asarray(inputs["spatial_b"], f32).reshape(2, 1024)),
        sink=np.asarray(inputs["sink"], f32),
        gtbT=np.ascontiguousarray(np.asarray(inputs["gate_b"], f32).reshape(2, 3, 16, 128).transpose(0, 3, 1, 2)
                                  .reshape(2, 128, 48)),
        ln_g=np.asarray(inputs["ln_g"], f32), ln_b=np.asarray(inputs["ln_b"], f32),
    )
    maps = []
    for c in range(NCORE):
        t0 = c * 2048
        m = dict(shared)
        m["xs"] = np.ascontiguousarray(xp[t0:t0 + 2560])
        m["posT"] = np.ascontiguousarray(pp[t0:t0 + 2560].reshape(20, 128).T)
        mk = np.stack([np.tile(tri_prev, (1, 4)), np.tile(tri_next, (1, 4)),
                       np.tile(allneg if c == 0 else tri_prev, (1, 4)),
                       np.tile(allneg if c == NCORE - 1 else tri_next, (1, 4))], axis=1)
        m["maskin"] = np.ascontiguousarray(mk.astype(f32))
        e = np.ones((128, 2), f32)
        if c == 0:
            e[:, 0] = 0.0
        if c == NCORE - 1:
            e[:, 1] = 0.0
        m["edgein"] = e
        maps.append(m)
    return maps


FUSED = True


def kernel(**inputs):
    if FUSED:
        nc, specs = build_nc()
        maps = _host_inputs(inputs, specs)
        res = run_bass_kernel_spmd(nc, maps, core_ids=list(range(NCORE)))
    else:
        nc0, sp0 = build_nc(layers=(0,), h1_kind="ExternalOutput")
        nc1, sp1 = build_nc(layers=(1,), h1_kind="ExternalInput")
        specs = dict(sp0)
        specs.update(sp1)
        maps = _host_inputs(inputs, specs)
        r0 = run_bass_kernel_spmd(nc0, maps, core_ids=list(range(NCORE)))
        maps1 = []
        for c in range(NCORE):
            m = dict(maps[c])
            m["h1s"] = np.asarray(r0.results[c]["h1s"], dtype=np.float32)
            maps1.append(m)
        res = run_bass_kernel_spmd(nc1, maps1, core_ids=list(range(NCORE)))
    outs = [np.asarray(r["out"], dtype=np.float32) for r in res.results]
    return np.concatenate(outs, axis=0).reshape(1, SEQ, D)
```
